# Optimizing a Trainium2 kernel written in Bass

```python
import jax, jax.numpy as jnp
from jax import lax
import numpy as np

D_MODEL = 4096
BATCH = 1
SEQ = 8192
DEPTH = 4

GRID_W = 64
CTX_LEN = 256
HEAD_DIM = 128
N_MIX_HEADS = D_MODEL // HEAD_DIM
A_Q_HEADS = N_MIX_HEADS // 2
A_KV_HEADS = A_Q_HEADS // 4
GQA_GROUP = A_Q_HEADS // A_KV_HEADS
B_GROUPS = N_MIX_HEADS // 2
C_GROUPS = 3 * N_MIX_HEADS // 4
D_GROUPS = N_MIX_HEADS // 4
CHUNK = 128
Q_BLOCK = 128
D_FF = 3 * D_MODEL // 2
ROPE_THETA = 10000.0
EPS = 1e-6
N_EVEN = (DEPTH + 1) // 2
N_ODD = DEPTH // 2

A_Q_W = A_Q_HEADS * HEAD_DIM
A_KV_W = A_KV_HEADS * HEAD_DIM
B_W = B_GROUPS * HEAD_DIM
C_W = C_GROUPS * HEAD_DIM
D_W = D_GROUPS * HEAD_DIM
EVEN_IN = A_Q_W + 2 * A_KV_W + 2 * B_W
EVEN_MIX = A_Q_W + B_W
ODD_IN = 3 * C_W + D_W
ODD_MIX = C_W + D_W

kernel_name = 'hybrid_diffusion_backbone_gqa_gmlp_shortconv_fourier'


def rmsnorm(x, g):
    x32 = x.astype(jnp.float32)
    y = x32 * lax.rsqrt(jnp.mean(x32 * x32, axis=-1, keepdims=True) + EPS)
    return y.astype(x.dtype) * g


def modulate(x, g, shift, scale):
    return rmsnorm(x, g) * (1.0 + scale) + shift


def dwconv3(x, w):
    xp = jnp.pad(x, ((0, 0), (1, 1), (0, 0)))
    return xp[:, :-2] * w[0] + xp[:, 1:-1] * w[1] + xp[:, 2:] * w[2]


def axial_rope_tables(n, dtype):
    rows = n // GRID_W
    row = jnp.repeat(jnp.arange(rows, dtype=jnp.float32), GRID_W)
    col = jnp.tile(jnp.arange(GRID_W, dtype=jnp.float32), rows)
    half = HEAD_DIM // 2
    inv_freq = ROPE_THETA ** (-jnp.arange(0, half, 2, dtype=jnp.float32) / half)
    ang = jnp.concatenate([row[:, None] * inv_freq, col[:, None] * inv_freq], axis=-1)
    return jnp.cos(ang).astype(dtype), jnp.sin(ang).astype(dtype)


def apply_rope(x, cos, sin):
    xr = x.reshape(x.shape[:-1] + (HEAD_DIM // 2, 2))
    x0, x1 = xr[..., 0], xr[..., 1]
    c = cos[None, :, None, :]
    s = sin[None, :, None, :]
    return jnp.stack([x0 * c - x1 * s, x0 * s + x1 * c], axis=-1).reshape(x.shape)


def attend(q, k, v):
    s = jnp.einsum('bqkgd,bskd->bkgqs', q, k, preferred_element_type=jnp.float32) * (HEAD_DIM ** -0.5)
    p = jax.nn.softmax(s, axis=-1).astype(v.dtype)
    return jnp.einsum('bkgqs,bskd->bqkgd', p, v)


def latent_attention(q, k_all, v_all):
    b, n = q.shape[:2]
    nb = n // Q_BLOCK
    qb = q.reshape(b, nb, Q_BLOCK, A_KV_HEADS, GQA_GROUP, HEAD_DIM).transpose(1, 0, 2, 3, 4, 5)
    o = lax.map(lambda qblk: attend(qblk, k_all, v_all), qb)
    return o.transpose(1, 0, 2, 3, 4, 5).reshape(b, n, A_Q_W)


def chunk_gmlp(u, v, norm_g, w_s, b_s):
    b, n, _ = u.shape
    u = jax.nn.gelu(u)
    v = rmsnorm(jax.nn.gelu(v).reshape(b, n, B_GROUPS, HEAD_DIM), norm_g.reshape(B_GROUPS, HEAD_DIM))
    v = v.reshape(b, n // CHUNK, CHUNK, B_GROUPS, HEAD_DIM)
    v = jnp.einsum('gpq,bnqgc->bnpgc', w_s, v) + b_s.T[:, :, None]
    return u * v.reshape(b, n, B_W)


def fourier_mix(f):
    b, n, _ = f.shape
    f32 = f.astype(jnp.float32).reshape(b, n, D_GROUPS, HEAD_DIM)
    y = jnp.fft.fft2(f32, axes=(1, 3), norm='ortho').real
    return y.reshape(b, n, D_W).astype(f.dtype)


def even_mixer(h, hc, w_in, w_out, q_g, k_g, gn_g, w_s, b_s, cos, sin, with_ctx_out):
    b, n, _ = h.shape
    lc = hc.shape[1]
    s1, s2, s3, s4 = A_Q_W, A_Q_W + A_KV_W, A_Q_W + 2 * A_KV_W, A_Q_W + 2 * A_KV_W + B_W
    q, k, v, u, gv = jnp.split(h @ w_in, [s1, s2, s3, s4], axis=-1)
    q = apply_rope(rmsnorm(q.reshape(b, n, A_Q_HEADS, HEAD_DIM), q_g), cos, sin)
    k = apply_rope(rmsnorm(k.reshape(b, n, A_KV_HEADS, HEAD_DIM), k_g), cos, sin)
    v = v.reshape(b, n, A_KV_HEADS, HEAD_DIM)
    if with_ctx_out:
        qc, kc, vc, uc, gvc = jnp.split(hc @ w_in, [s1, s2, s3, s4], axis=-1)
    else:
        kc, vc = jnp.split(hc @ w_in[:, s1:s3], 2, axis=-1)
    kc = rmsnorm(kc.reshape(b, lc, A_KV_HEADS, HEAD_DIM), k_g)
    vc = vc.reshape(b, lc, A_KV_HEADS, HEAD_DIM)
    k_all = jnp.concatenate([kc, k], axis=1)
    v_all = jnp.concatenate([vc, v], axis=1)
    attn = latent_attention(q.reshape(b, n, A_KV_HEADS, GQA_GROUP, HEAD_DIM), k_all, v_all)
    y = jnp.concatenate([attn, chunk_gmlp(u, gv, gn_g, w_s, b_s)], axis=-1) @ w_out
    if not with_ctx_out:
        return y, None
    qc = rmsnorm(qc.reshape(b, lc, A_KV_HEADS, GQA_GROUP, HEAD_DIM), q_g)
    attn_c = attend(qc, kc, vc).reshape(b, lc, A_Q_W)
    yc = jnp.concatenate([attn_c, chunk_gmlp(uc, gvc, gn_g, w_s, b_s)], axis=-1) @ w_out
    return y, yc


def odd_mixer(h, w_in, w_out, conv_w):
    bg, cg, hx, f = jnp.split(h @ w_in, [C_W, 2 * C_W, 3 * C_W], axis=-1)
    y_c = bg * dwconv3(cg * hx, conv_w)
    return jnp.concatenate([y_c, fourier_mix(f)], axis=-1) @ w_out


def conv_ffn(h, w_up, conv_w, conv_b, w_down):
    a = dwconv3(h @ w_up, conv_w) + conv_b
    g, u = jnp.split(a, 2, axis=-1)
    return (jax.nn.silu(g) * u) @ w_down


def setup_inputs(seed: int = 0) -> dict:
    key = jax.random.key(seed)
    ks = jax.random.split(key, 24)
    D = D_MODEL

    def nrm(k, shape, s):
        return jax.random.normal(k, shape, jnp.float32) * s

    return {
        'x': nrm(ks[0], (BATCH, SEQ, D), 1.0),
        'c': nrm(ks[1], (BATCH, D), 1.0),
        'ctx': nrm(ks[2], (BATCH, CTX_LEN, D), 1.0),
        'c_ctx': nrm(ks[3], (D,), 1.0),
        'w_ada': nrm(ks[4], (DEPTH, D, 6 * D), 0.5 * D ** -0.5),
        'b_ada': nrm(ks[5], (DEPTH, 6 * D), 0.02),
        'norm1_g': 1.0 + nrm(ks[6], (DEPTH, D), 0.02),
        'norm2_g': 1.0 + nrm(ks[7], (DEPTH, D), 0.02),
        'w_ffn_up': nrm(ks[8], (DEPTH, D, 2 * D_FF), D ** -0.5),
        'ffn_conv_w': nrm(ks[9], (DEPTH, 3, 2 * D_FF), 3 ** -0.5),
        'ffn_conv_b': nrm(ks[10], (DEPTH, 2 * D_FF), 0.02),
        'w_ffn_down': nrm(ks[11], (DEPTH, D_FF, D), D_FF ** -0.5),
        'w_in_a': nrm(ks[12], (N_EVEN, D, EVEN_IN), D ** -0.5),
        'w_out_a': nrm(ks[13], (N_EVEN, EVEN_MIX, D), EVEN_MIX ** -0.5),
        'q_norm_g': 1.0 + nrm(ks[14], (N_EVEN, HEAD_DIM), 0.02),
        'k_norm_g': 1.0 + nrm(ks[15], (N_EVEN, HEAD_DIM), 0.02),
        'gmlp_norm_g': 1.0 + nrm(ks[16], (N_EVEN, B_W), 0.02),
        'gmlp_w_s': nrm(ks[17], (N_EVEN, B_GROUPS, CHUNK, CHUNK), CHUNK ** -0.5),
        'gmlp_b_s': nrm(ks[18], (N_EVEN, B_GROUPS, CHUNK), 0.02),
        'w_in_c': nrm(ks[19], (N_ODD, D, ODD_IN), D ** -0.5),
        'w_out_c': nrm(ks[20], (N_ODD, ODD_MIX, D), ODD_MIX ** -0.5),
        'conv_w_c': nrm(ks[21], (N_ODD, 3, C_W), 3 ** -0.5),
    }


def reference(x, c, ctx, c_ctx, w_ada, b_ada, norm1_g, norm2_g, w_ffn_up, ffn_conv_w, ffn_conv_b,
              w_ffn_down, w_in_a, w_out_a, q_norm_g, k_norm_g, gmlp_norm_g, gmlp_w_s, gmlp_b_s,
              w_in_c, w_out_c, conv_w_c):
    n = x.shape[1]
    cos, sin = axial_rope_tables(n, x.dtype)
    xc = ctx
    s_lat = jax.nn.silu(c)
    s_ctx = jax.nn.silu(c_ctx)
    for i in range(DEPTH):
        ctx_needed_later = any(j % 2 == 0 for j in range(i + 1, DEPTH))
        sh1, sc1, g1, sh2, sc2, g2 = jnp.split((s_lat @ w_ada[i] + b_ada[i])[:, None, :], 6, axis=-1)
        if i % 2 == 0 or ctx_needed_later:
            csh1, csc1, cg1, csh2, csc2, cg2 = jnp.split(s_ctx @ w_ada[i] + b_ada[i], 6, axis=-1)
        h = modulate(x, norm1_g[i], sh1, sc1)
        if i % 2 == 0:
            e = i // 2
            hc = modulate(xc, norm1_g[i], csh1, csc1)
            y, yc = even_mixer(h, hc, w_in_a[e], w_out_a[e], q_norm_g[e], k_norm_g[e], gmlp_norm_g[e],
                               gmlp_w_s[e], gmlp_b_s[e], cos, sin, ctx_needed_later)
            x = x + g1 * y
            if ctx_needed_later:
                xc = xc + cg1 * yc
        else:
            o = i // 2
            x = x + g1 * odd_mixer(h, w_in_c[o], w_out_c[o], conv_w_c[o])
            if ctx_needed_later:
                hc = modulate(xc, norm1_g[i], csh1, csc1)
                xc = xc + cg1 * odd_mixer(hc, w_in_c[o], w_out_c[o], conv_w_c[o])
        x = x + g2 * conv_ffn(modulate(x, norm2_g[i], sh2, sc2), w_ffn_up[i], ffn_conv_w[i],
                              ffn_conv_b[i], w_ffn_down[i])
        if ctx_needed_later:
            xc = xc + cg2 * conv_ffn(modulate(xc, norm2_g[i], csh2, csc2), w_ffn_up[i], ffn_conv_w[i],
                                     ffn_conv_b[i], w_ffn_down[i])
    return x
```

```python
from contextlib import ExitStack
import numpy as np
import ml_dtypes
import concourse.bass as bass
import concourse.mybir as mybir
from concourse.bass_utils import run_bass_kernel_spmd

F32 = mybir.dt.float32
BF16 = mybir.dt.bfloat16
AF = mybir.ActivationFunctionType
ALU = mybir.AluOpType
NCORES = 8
EPS = 1e-6
ROPE_THETA = 10000.0


class Cfg:
    def __init__(s, D=4096, SEQ=8192, CTX=256, GRID_W=64, DEPTH=4):
        s.D, s.SEQ, s.CTX, s.GRID_W, s.L = D, SEQ, CTX, GRID_W, DEPTH
        s.KC = D // 128
        NH = D // 128
        s.AQ = NH // 2; s.AKV = s.AQ // 4; s.BG = NH // 2; s.CG = 3 * NH // 4; s.DG = NH // 4
        s.DFF = 3 * D // 2; s.FC = s.DFF // 128
        s.EVEN_IN = (s.AQ + 2 * s.AKV + 2 * s.BG) * 128
        s.ODD_IN = (3 * s.CG + s.DG) * 128
        s.TL = SEQ // NCORES; s.TC = CTX
        s.TP = s.TL + s.TC + 4
        s.L0 = 1; s.RH = s.TL + 1; s.C0 = s.TL + 3
        s.NE = (DEPTH + 1) // 2; s.NO = DEPTH // 2
        s.ACH = 6 * s.KC // NCORES


def ctiles(lo, hi, maxw=512):
    n = -(-(hi - lo) // maxw)
    w = -(-(hi - lo) // n)
    return [(a, min(a + w, hi)) for a in range(lo, hi, w)]


class Op:
    __slots__ = ("eng", "fn", "r", "w", "sem", "incv", "deps", "needs", "count")


class Prog:
    ENG = ("pe", "act", "dve", "pool", "sp")

    def __init__(s):
        s.ops = []
        s.ncc = 0

    def op(s, eng, fn, r=(), w=(), stream=None, cc=False):
        o = Op()
        o.eng, o.fn, o.r, o.w = eng, fn, tuple(r) + ("G",), tuple(w)
        if cc:
            o.sem = "cc"; o.incv = 1; o.w = o.w + ("CCSER",)
        elif stream is not None:
            o.sem = "d_" + stream; o.incv = 16
        else:
            o.sem = eng; o.incv = 1
        o.deps = None; o.needs = bool(stream is not None or cc); o.count = 0
        s.ops.append(o)
        return o

    def dma(s, q, out, in_, r, w, stream, slow=False):
        if slow:
            return s.op(q, lambda e: e.dma_start(out=out, in_=in_, allow_slow_non_contiguous=True), r, w, stream=stream)
        return s.op(q, lambda e: e.dma_start(out=out, in_=in_), r, w, stream=stream)

    def barrier(s, scratch):
        s.op("dve", lambda e: e.memset(scratch, 0.0), r=(), w=("G",))

    def analyze(s):
        lastw = {}
        readers = {}
        for i, o in enumerate(s.ops):
            deps = {}
            def add(j):
                d = s.ops[j]
                if d.sem == "pe" and o.sem == "pe":
                    return
                if d.sem not in deps or deps[d.sem] < j:
                    deps[d.sem] = j
            for k in o.r:
                if k in lastw:
                    add(lastw[k])
            for k in o.w:
                if k in lastw:
                    add(lastw[k])
                for j in readers.get(k, {}).values():
                    if j != i:
                        add(j)
            for k in o.w:
                lastw[k] = i
                readers[k] = {}
            for k in o.r:
                if k not in o.w:
                    readers.setdefault(k, {})[o.sem] = i
            o.deps = list(deps.values())
            for j in o.deps:
                s.ops[j].needs = True
        cum = {}
        for o in s.ops:
            if o.needs:
                cum[o.sem] = cum.get(o.sem, 0) + o.incv
                o.count = cum[o.sem]
        return sorted(cum.keys())

    def emit(s, nc):
        semnames = s.analyze()
        mx = {}
        for o in s.ops:
            mx[o.sem] = max(mx.get(o.sem, 0), o.count)
        print("nsems", len(semnames), "nops", len(s.ops), sorted(mx.items(), key=lambda kv: -kv[1])[:12])
        with ExitStack() as es:
            sems = {n: es.enter_context(nc.semaphore("s_" + n)) for n in semnames}
            blk = es.enter_context(nc.Block())
            per = {e: [] for e in s.ENG}
            for o in s.ops:
                per[o.eng].append(o)

            def run(eng_name):
                def f(e):
                    waited = {}
                    for o in per[eng_name]:
                        for j in o.deps:
                            d = s.ops[j]
                            if waited.get(d.sem, 0) < d.count:
                                e.wait_ge(sems[d.sem], d.count)
                                waited[d.sem] = d.count
                        ins = o.fn(e)
                        if o.needs:
                            ins.then_inc(sems[o.sem], o.incv)
                return f
            blk.tensor(run("pe"))
            blk.scalar(run("act"))
            blk.vector(run("dve"))
            blk.gpsimd(run("pool"))
            blk.sync(run("sp"))


def build(cfg):
    c = cfg
    D, KC, TL, TC, TP, L = c.D, c.KC, c.TL, c.TC, c.TP, c.L
    nc = bass.Bass("TRN2", target_bir_lowering=False)
    P = Prog()

    def din(name, shape, dt=F32):
        return nc.dram_tensor(name, list(shape), dt, kind="ExternalInput")

    def dtmp(name, shape, dt=F32):
        return nc.dram_tensor(name, list(shape), dt)

    x_in = din("x_loc", [TL, D]); ctx_in = din("ctx", [TC, D])
    cvec = din("cvec", [2 * KC, 128])
    wada = din("w_ada", [L * D, c.ACH * 128]); bada = din("b_ada", [1, L * c.ACH * 128])
    n1g = din("norm1_g", [L * KC, 128]); n2g = din("norm2_g", [L * KC, 128])
    fcw = din("ffn_conv_w", [L * 3 * 2 * c.FC, 128]); fcb = din("ffn_conv_b", [L * 2 * c.FC, 128])
    ccw = din("conv_w_c", [c.NO * 3 * c.CG, 128]); gng = din("gmlp_norm_g", [c.NE * c.BG, 128])
    qkg = din("qk_norm_g", [2 * c.NE, 128])
    gws = din("gmlp_w_s", [c.NE * c.BG * 128, 128]); gbs = din("gmlp_b_s", [c.NE, c.BG * 128])
    ropec = din("rope_cos", [128, TP]); ropes = din("rope_sin", [128, TP])
    csmat = din("csmat", [128, 256])
    dftc = din("dft_c", [c.SEQ, TL], BF16); dfts = din("dft_s", [c.SEQ, TL], BF16)
    dftcc = din("dft_cc", [TC, TC], BF16); dftcs = din("dft_cs", [TC, TC], BF16)
    masks = din("masks", [128, 2 * NCORES + 2])
    ident_in = din("ident", [128, 128]); pmat_in = din("pmat", [128, 128])
    GWMAX = 1024

    def wdecl(name, n, K, N):
        out = []
        RB = K // NCORES // 128
        assert RB * 128 * NCORES == K
        for i in range(n):
            src = din("%s%d" % (name, i), [K // NCORES, N])
            groups = []
            for g0 in range(0, N, GWMAX):
                gw = min(GWMAX, N - g0)
                rbs = [(dtmp("%s%d_s%d_%d" % (name, i, g0, rb), [128, gw], BF16),
                        dtmp("%s%d_f%d_%d" % (name, i, g0, rb), [NCORES * 128, gw], BF16)) for rb in range(RB)]
                groups.append((g0, gw, rbs))
            out.append((src, groups, K, N))
        return out
    W_in_a = wdecl("w_in_a", c.NE, D, c.EVEN_IN); W_out_a = wdecl("w_out_a", c.NE, D, D)
    W_in_c = wdecl("w_in_c", c.NO, D, c.ODD_IN); W_out_c = wdecl("w_out_c", c.NO, D, D)
    W_up = wdecl("w_ffn_up", L, D, 2 * c.DFF); W_down = wdecl("w_ffn_down", L, c.DFF, D)
    y_out = nc.dram_tensor("y", [TL, D], F32, kind="ExternalOutput")

    xT = dtmp("xT", [D, TP])
    projT = dtmp("projT", [max(c.EVEN_IN, c.ODD_IN, 2 * c.DFF), TP])
    mixT = dtmp("mixT", [D, TP], BF16)
    hidT = dtmp("hidT", [c.DFF, TP], BF16)
    qT = dtmp("qT", [(c.AQ + c.AKV) * 128, TP], BF16)
    k_src = [dtmp("k_src%d" % h, [128, TL], BF16) for h in range(c.AKV)]
    k_all = [dtmp("k_all%d" % h, [NCORES * 128, TL], BF16) for h in range(c.AKV)]
    NB = TL // 128
    VW = c.AKV * 129
    VW += VW % 2
    v_src = [dtmp("v_src%d" % i, [128, VW], BF16) for i in range(NB)]
    v_all = [dtmp("v_all%d" % i, [NCORES * 128, VW], BF16) for i in range(NB)]
    v_ctx = dtmp("v_ctx", [TC, VW], BF16)
    GH = min(4, c.DG); NGH = c.DG // GH
    ab_src = [[dtmp("ab_src%d_%d" % (i, gh), [128, GH * 256], BF16) for gh in range(NGH)] for i in range(NB)]
    ab_all = [[dtmp("ab_all%d_%d" % (i, gh), [NCORES * 128, GH * 256], BF16) for gh in range(NGH)] for i in range(NB)]
    ab_ctx = dtmp("ab_ctx", [TC, c.DG * 256], BF16)
    ada_src = dtmp("ada_src", [128, L * c.ACH * 2]); ada_all = dtmp("ada_all", [NCORES * 128, L * c.ACH * 2])
    halo_src = dtmp("halo_src", [128, KC * 2]); halo_all = dtmp("halo_all", [NCORES * 128, KC * 2])

    es = ExitStack()
    with es:
        def sb(name, shape, dt=F32):
            return es.enter_context(nc.sbuf_tensor("sb_" + name, list(shape), dt))
        AR_N = max(KC * TP, c.FC * ((TP + 1) // 2 + 1), 4 * 2048 + 2 * 1024)
        arena = sb("arena", [128, AR_N], BF16)
        NF = 8
        Fb = [sb("f%d" % i, [128, TP]) for i in range(NF)]
        KCM = max(KC, c.FC)
        wbuf = [sb("wbuf%d" % i, [128, KCM, 128], BF16) for i in range(2)]
        nvec = L * KC
        n1gT = sb("n1gT", [128, nvec]); n2gT = sb("n2gT", [128, nvec])
        fcwT = sb("fcwT", [128, L * 3 * 2 * c.FC]); fcbT = sb("fcbT", [128, L * 2 * c.FC])
        ccwT = sb("ccwT", [128, c.NO * 3 * c.CG]); gngT = sb("gngT", [128, c.NE * c.BG])
        qkgT = sb("qkgT", [128, 2 * c.NE]); sT = sb("sT", [128, KC, 2])
        adaF = sb("adaF", [128, L, 6 * KC, 2])
        mod = sb("mod", [128, 6, 2, KC])
        ident = sb("ident", [128, 128]); pmatg = sb("pmatg", [128, 2, 128]); pmat = sb("pmat", [128, 128])
        ones_bf = sb("ones_bf", [128, 128], BF16); ones_f = sb("ones_f", [128, 128])
        cosT = sb("cosT", [128, TP]); sinT = sb("sinT", [128, TP])
        cs_bf = sb("cs_bf", [128, 256], BF16)
        wsT = sb("wsT", [128, c.BG, 128], BF16); bsb = sb("bsb", [128, c.BG * 128])
        msk = sb("msk", [128, 2 * NCORES + 2])
        xhalo = sb("xhalo", [128, KC, 2]); xedge = sb("xedge", [128, KC, 2]); hall = sb("hall", [128, NCORES, KC, 2])
        rstd = sb("rstd", [128, TP])
        small = sb("small", [128, 512])
        scr = sb("scr", [128, 8])
        PS = [es.enter_context(nc.psum_tensor("ps%d" % i, [128, 512], F32)) for i in range(8)]

        def act(out, in_, func, r, w, bias=None, scale=None):
            kw = {}
            if bias is not None: kw["bias"] = bias
            if scale is not None: kw["scale"] = scale
            P.op("act", lambda e: e.activation(out=out, in_=in_, func=func, **kw), r, w)

        def tt(eng, out, a, b, op, r, w):
            P.op(eng, lambda e: e.tensor_tensor(out=out, in0=a, in1=b, op=op), r, w)

        def ts(eng, out, a, s1, s2, op0, op1, r, w):
            if op1 is None:
                P.op(eng, lambda e: e.tensor_scalar(out=out, in0=a, scalar1=s1, scalar2=None, op0=op0), r, w)
            else:
                P.op(eng, lambda e: e.tensor_scalar(out=out, in0=a, scalar1=s1, scalar2=s2, op0=op0, op1=op1), r, w)

        def stt(eng, out, a, sc, b, op0, op1, r, w):
            P.op(eng, lambda e: e.scalar_tensor_tensor(out=out, in0=a, scalar=sc, in1=b, op0=op0, op1=op1), r, w)

        def cp(eng, out, in_, r, w):
            if eng == "act":
                P.op("act", lambda e: e.copy(out=out, in_=in_), r, w)
            else:
                P.op(eng, lambda e: e.tensor_copy(out=out, in_=in_), r, w)

        def mm(out, lhsT, rhs, start, stop, r, w):
            P.op("pe", lambda e: e.matmul(out, lhsT, rhs, start=start, stop=stop), r, w)

        def tr(out, in_, idn, r, w):
            P.op("pe", lambda e: e.transpose(out, in_, idn), r, w)

        def memset(eng, ap, v, w):
            P.op(eng, lambda e: e.memset(ap, v), (), w)

        cnt = [0]

        def uid():
            cnt[0] += 1
            return cnt[0]

        def ld(dst, src, key):
            P.dma("sp", dst, src, r=(), w=(key,), stream="c_" + key)
        ld(ident[:, :], ident_in[:, :], "ident"); ld(pmat[:, :], pmat_in[:, :], "pmat")
        ld(cosT[:, :], ropec[:, :], "cosT"); ld(sinT[:, :], ropes[:, :], "sinT")
        ld(msk[:, :], masks[:, :], "msk"); ld(Fb[0][:, 0:256], csmat[:, :], "F0")
        cp("dve", cs_bf[:, :], Fb[0][:, 0:256], ("F0",), ("cs_bf",))
        memset("dve", ones_f[:, :], 1.0, ("ones_f",)); memset("dve", ones_bf[:, :], 1.0, ("ones_bf",))
        memset("dve", xhalo[:, :, :], 0.0, ("xhalo",))

        def vecT(dst, dst_key, src, R):
            for r0 in range(0, R, 128):
                rn = min(128, R - r0)
                P.dma("sp", Fb[1][0:rn, 0:128], src[r0:r0 + rn, :], r=(), w=("F1",), stream="c1")
                tr(PS[7][:, 0:rn], Fb[1][0:rn, 0:128], ident[0:rn, 0:rn], ("F1", "ident"), ("ps7",))
                cp("dve", dst[:, r0:r0 + rn], PS[7][:, 0:rn], ("ps7",), (dst_key,))
        vecT(n1gT, "n1gT", n1g, L * KC); vecT(n2gT, "n2gT", n2g, L * KC)
        vecT(fcwT, "fcwT", fcw, L * 6 * c.FC); vecT(fcbT, "fcbT", fcb, L * 2 * c.FC)
        vecT(ccwT, "ccwT", ccw, c.NO * 3 * c.CG); vecT(gngT, "gngT", gng, c.NE * c.BG)
        vecT(qkgT, "qkgT", qkg, 2 * c.NE)
        sraw = Fb[2]
        vecT(sraw, "F2", cvec, 2 * KC)
        for s_ in range(2):
            act(sT[:, :, s_], sraw[:, s_ * KC:(s_ + 1) * KC], AF.Silu, ("F2",), ("sT",))

        adast = Fb[3]
        wada_v = wada.ap().rearrange("(l kc p) n -> l p kc n", l=L, p=128)
        wa = [arena[:, i * KC * 256:(i + 1) * KC * 256].bitcast(F32).rearrange("p (k n) -> p k n", k=KC) for i in range(2)]
        it = 0
        for l in range(L):
            for jj in range(c.ACH):
                slot = it % 2; it += 1
                P.dma("sp", wa[slot], wada_v[l][:, :, jj * 128:(jj + 1) * 128], (), ("wa%d" % slot,), "wa%d" % slot)
                po = PS[6][:, 2 * (it % 64):2 * (it % 64) + 2]
                for kc in range(KC):
                    mm(po, wa[slot][:, kc, :], sT[:, kc, :], kc == 0, False, ("wa%d" % slot, "sT"), ("ps6",))
                off = (l * c.ACH + jj) * 128
                bsl = small[0:1, slot * 128:(slot + 1) * 128]
                P.dma("sp", bsl, bada[0:1, off:off + 128], (), ("bsl%d" % slot,), "bsl%d" % slot)
                mm(po, bsl, ones_f[0:1, 0:2], False, True, ("bsl%d" % slot, "ones_f"), ("ps6",))
                cp("dve", adast[:, (l * c.ACH + jj) * 2:(l * c.ACH + jj) * 2 + 2], po, ("ps6",), ("F3",))
        NA = L * c.ACH * 2
        P.dma("pool", ada_src[:, :], adast[:, 0:NA], ("F3",), ("ada_src",), "st0")
        P.op("pool", lambda e: e.collective_compute("AllGather", ALU.bypass, replica_groups=[list(range(NCORES))],
                                                    ins=[ada_src[:, :]], outs=[ada_all[:, :]]), ("ada_src",), ("ada_all",), cc=True)
        for r_ in range(NCORES):
            P.dma("sp", adaF[:, :, r_ * c.ACH:(r_ + 1) * c.ACH, :].rearrange("p l j s -> p l (j s)"),
                  ada_all[r_ * 128:(r_ + 1) * 128, :].rearrange("p (l n) -> p l n", l=L), ("ada_all",), ("adaF",), "ld_ada")
        P.barrier(scr[:, 0:1])

        CB = min(1024, TP)
        castb = [arena[:, i * CB:(i + 1) * CB] for i in range(4)]
        cast_i = [0]

        def prep_weight(went):
            src, groups, K, N = went
            for (g0, gw, rbs) in groups:
                for rb, (wsrc, wfull) in enumerate(rbs):
                    for c0 in range(0, gw, CB):
                        cw = min(CB, gw - c0)
                        i = cast_i[0]; cast_i[0] += 1
                        fs = 6 + (i % 2); cs = i % 4
                        P.dma("sp", Fb[fs][:, 0:cw], src[rb * 128:(rb + 1) * 128, g0 + c0:g0 + c0 + cw], (), ("F%d" % fs,), "ldF%d" % fs)
                        eng = ("act", "dve")[i % 2]
                        cp(eng, castb[cs][:, 0:cw], Fb[fs][:, 0:cw], ("F%d" % fs,), ("cast%d" % cs,))
                        P.dma("pool", wsrc[:, c0:c0 + cw], castb[cs][:, 0:cw], ("cast%d" % cs,), (wsrc.name,), "stc%d" % cs)
                    P.op("pool", lambda e, a=wsrc, b=wfull: e.collective_compute(
                        "AllGather", ALU.bypass, replica_groups=[list(range(NCORES))], ins=[a[:, :]], outs=[b[:, :]]),
                        (wsrc.name,), (wfull.name,), cc=True)

        for lst in (W_in_a, W_out_a, W_in_c, W_out_c, W_up, W_down):
            for went in lst:
                prep_weight(went)

        def load_xT(src, ntok, col0):
            for tb in range(0, ntok, 128):
                tn = min(128, ntok - tb)
                for kb in range(0, KC, 2):
                    P.dma("sp", Fb[0][0:tn, 0:256], src[tb:tb + tn, kb * 128:kb * 128 + 256], (), ("F0",), "ldF0")
                    for k in range(2):
                        pi = uid() % 2
                        tr(PS[pi][:, 0:tn], Fb[0][0:tn, k * 128:(k + 1) * 128], ident[0:tn, 0:tn], ("F0", "ident"), ("ps%d" % pi,))
                        si = 1 + uid() % 2
                        cp("dve" if k % 2 else "act", Fb[si][:, 0:tn], PS[pi][:, 0:tn], ("ps%d" % pi,), ("F%d" % si,))
                        P.dma("pool", xT[(kb + k) * 128:(kb + k + 1) * 128, col0 + tb:col0 + tb + tn], Fb[si][:, 0:tn],
                              ("F%d" % si,), ("xT%d" % (kb + k),), "stF%d" % si)
                        if src is x_in:
                            if tb == 0:
                                cp("dve", xedge[:, kb + k, 0:1], Fb[si][:, 0:1], ("F%d" % si,), ("xedge",))
                            if tb + tn == ntok:
                                cp("dve", xedge[:, kb + k, 1:2], Fb[si][:, tn - 1:tn], ("F%d" % si,), ("xedge",))
        memset("dve", Fb[3][:, :], 0.0, ("F3",))
        for k in range(KC):
            P.dma("pool", xT[k * 128:(k + 1) * 128, :], Fb[3][:, :], ("F3",), ("xT%d" % k,), "stz")
        P.barrier(scr[:, 0:1])
        load_xT(x_in, TL, c.L0)
        load_xT(ctx_in, TC, c.C0)
        P.barrier(scr[:, 0:1])

        def halo_exchange():
            P.dma("pool", halo_src[:, :], xedge[:, :, :].rearrange("p k j -> p (k j)"), ("xedge",), ("halo_src",), "st0")
            P.op("pool", lambda e: e.collective_compute("AllGather", ALU.bypass, replica_groups=[list(range(NCORES))],
                                                        ins=[halo_src[:, :]], outs=[halo_all[:, :]]), ("halo_src",), ("halo_all",), cc=True)
            P.dma("sp", hall[:, :, :, :].rearrange("p r k j -> p r (k j)"), halo_all.ap().rearrange("(r p) n -> p r n", p=128),
                  ("halo_all",), ("hall",), "ld_hall")
            for side, j in ((0, 1), (1, 0)):
                for r_ in range(NCORES):
                    mcol = msk[:, side * NCORES + r_:side * NCORES + r_ + 1]
                    if r_ == 0:
                        ts("dve", xhalo[:, :, side], hall[:, r_, :, j], mcol, None, ALU.mult, None, ("hall", "msk"), ("xhalo",))
                    else:
                        stt("dve", xhalo[:, :, side], hall[:, r_, :, j], mcol, xhalo[:, :, side], ALU.mult, ALU.add,
                            ("hall", "msk", "xhalo"), ("xhalo",))

        def make_mod(l):
            for s_ in range(2):
                sec = lambda i: adaF[:, l, i * KC:(i + 1) * KC, s_]
                stt("dve", mod[:, 0, s_, :], sec(1), 1.0, n1gT[:, l * KC:(l + 1) * KC], ALU.add, ALU.mult, ("adaF", "n1gT"), ("mod",))
                cp("dve", mod[:, 1, s_, :], sec(0), ("adaF",), ("mod",))
                cp("dve", mod[:, 2, s_, :], sec(2), ("adaF",), ("mod",))
                stt("dve", mod[:, 3, s_, :], sec(4), 1.0, n2gT[:, l * KC:(l + 1) * KC], ALU.add, ALU.mult, ("adaF", "n2gT"), ("mod",))
                cp("dve", mod[:, 4, s_, :], sec(3), ("adaF",), ("mod",))
                cp("dve", mod[:, 5, s_, :], sec(5), ("adaF",), ("mod",))

        inT = arena[:, 0:KC * TP].rearrange("p (k t) -> p k t", k=KC)
        TT = ctiles(0, TP)

        def colsumsq_finish(ncols_tiles, nfeat, ps_ids):
            for ti, (a, b) in enumerate(ncols_tiles):
                act(rstd[:, a:b], PS[ps_ids[ti]][:, 0:b - a], AF.Sqrt, ("ps%d" % ps_ids[ti],), ("rstd",), bias=EPS_T[:, 0:1], scale=1.0 / nfeat)
            P.op("dve", lambda e: e.reciprocal(out=rstd[:, :], in_=rstd[:, :]), ("rstd",), ("rstd",))

        EPS_T = sb("eps_t", [128, 1])
        memset("dve", EPS_T[:, :], EPS, ("eps_t",))

        def norm_mod(which):
            ai, bi = (0, 1) if which == 1 else (3, 4)
            for k in range(KC):
                fi = uid() % 2
                P.dma("sp", Fb[fi][:, :], xT[k * 128:(k + 1) * 128, :], ("xT%d" % k,), ("F%d" % fi,), "ldF%d" % fi)
                cp("dve", Fb[fi][:, 0:1], xhalo[:, k, 0:1], ("xhalo",), ("F%d" % fi,))
                cp("dve", Fb[fi][:, c.RH:c.RH + 1], xhalo[:, k, 1:2], ("xhalo",), ("F%d" % fi,))
                sq = arena[:, KC * TP - TP:KC * TP] if False else None
                si = 2 + uid() % 2
                sqb = Fb[si][:, :].bitcast(BF16)[:, 0:TP]
                act(sqb, Fb[fi][:, :], AF.Square, ("F%d" % fi,), ("F%d" % si,))
                for ti, (a, b) in enumerate(TT):
                    mm(PS[ti][:, 0:b - a], ones_bf[:, :], sqb[:, a:b], k == 0, k == KC - 1, ("ones_bf", "F%d" % si), ("ps%d" % ti,))
            colsumsq_finish(TT, D, list(range(len(TT))))
            for k in range(KC):
                fi = uid() % 2
                P.dma("sp", Fb[fi][:, :], xT[k * 128:(k + 1) * 128, :], ("xT%d" % k,), ("F%d" % fi,), "ldF%d" % fi)
                cp("dve", Fb[fi][:, 0:1], xhalo[:, k, 0:1], ("xhalo",), ("F%d" % fi,))
                cp("dve", Fb[fi][:, c.RH:c.RH + 1], xhalo[:, k, 1:2], ("xhalo",), ("F%d" % fi,))
                tt("dve", Fb[fi][:, :], Fb[fi][:, :], rstd[:, :], ALU.mult, ("F%d" % fi, "rstd"), ("F%d" % fi,))
                act(inT[:, k, 0:c.RH + 1], Fb[fi][:, 0:c.RH + 1], AF.Identity, ("F%d" % fi, "mod"), ("inT",),
                    bias=mod[:, bi, 0, k:k + 1], scale=mod[:, ai, 0, k:k + 1])
                act(inT[:, k, c.C0:c.C0 + TC], Fb[fi][:, c.C0:c.C0 + TC], AF.Identity, ("F%d" % fi, "mod"), ("inT",),
                    bias=mod[:, bi, 1, k:k + 1], scale=mod[:, ai, 1, k:k + 1])
            memset("dve", inT[:, :, c.RH + 1:c.C0], 0.0, ("inT",))
            memset("dve", inT[:, :, TP - 1:TP], 0.0, ("inT",))
            ts("dve", inT[:, :, 0:1], inT[:, :, 0:1], msk[:, 2 * NCORES:2 * NCORES + 1], None, ALU.mult, None, ("inT", "msk"), ("inT",))
            ts("dve", inT[:, :, c.RH:c.RH + 1], inT[:, :, c.RH:c.RH + 1], msk[:, 2 * NCORES + 1:2 * NCORES + 2], None, ALU.mult, None, ("inT", "msk"), ("inT",))

        lin_it = [0]
        lin_epoch = [0]

        def linear(went, kcn, inv, col_tiles, epilogue, in_key="inT"):
            src, groups, K, N = went
            RB = kcn // NCORES
            for (g0, gw, rbs) in groups:
                for m0 in range(0, gw, 128):
                    m = (g0 + m0) // 128
                    slot = lin_it[0] % 2; lin_it[0] += 1
                    for rb, (wsrc, wfull) in enumerate(rbs):
                        P.dma("sp", wbuf[slot][:, rb * NCORES:(rb + 1) * NCORES, :],
                              wfull.ap().rearrange("(r p) n -> p r n", p=128)[:, :, m0:m0 + 128], (wfull.name,), ("wbuf%d" % slot,), "wb%d_%d" % (slot, lin_epoch[0]))
                    pbase = 0 if slot == 0 else 4
                    for ti, (a, b) in enumerate(col_tiles):
                        pi = pbase + ti
                        for j in range(kcn):
                            kc_ = (j % NCORES) * RB + j // NCORES
                            mm(PS[pi][:, 0:b - a], wbuf[slot][:, j, :], inv[:, kc_, a:b], j == 0, j == kcn - 1,
                               ("wbuf%d" % slot, in_key), ("ps%d" % pi,))
                    epilogue(m, [(pbase + ti, a, b) for ti, (a, b) in enumerate(col_tiles)])

        def ep_store(dst):
            def f(m, tiles):
                fi = 4 + uid() % 2
                for j, (pi, a, b) in enumerate(tiles):
                    cp("act" if j % 2 == 0 else "dve", Fb[fi][:, a:b], PS[pi][:, 0:b - a], ("ps%d" % pi,), ("F%d" % fi,))
                lo, hi = tiles[0][1], tiles[-1][2]
                P.dma("pool", dst[m * 128:(m + 1) * 128, lo:hi], Fb[fi][:, lo:hi], ("F%d" % fi,), ("%s%d" % (dst.name, m),), "stF%d" % fi)
            return f

        def ep_resid(gidx, final=False):
            def f(m, tiles):
                fi = 4 + uid() % 2
                lo, hi = tiles[0][1], tiles[-1][2]
                P.dma("sp", Fb[fi][:, lo:hi], xT[m * 128:(m + 1) * 128, lo:hi], ("xT%d" % m,), ("F%d" % fi,), "ldF%d" % fi)
                for (pi, a, b) in tiles:
                    for (ra, rb, s_) in ((0, c.RH + 1, 0), (c.RH + 1, TP, 1)):
                        aa, bb = max(a, ra), min(b, rb)
                        if aa < bb:
                            stt("dve", Fb[fi][:, aa:bb], PS[pi][:, aa - a:bb - a], mod[:, gidx, s_, m:m + 1], Fb[fi][:, aa:bb],
                                ALU.mult, ALU.add, ("ps%d" % pi, "mod", "F%d" % fi), ("F%d" % fi,))
                P.dma("pool", xT[m * 128:(m + 1) * 128, lo:hi], Fb[fi][:, lo:hi], ("F%d" % fi,), ("xT%d" % m,), "stF%d" % fi)
                if lo <= c.L0 < hi:
                    cp("dve", xedge[:, m, 0:1], Fb[fi][:, c.L0:c.L0 + 1], ("F%d" % fi,), ("xedge",))
                if lo <= TL < hi:
                    cp("dve", xedge[:, m, 1:2], Fb[fi][:, TL:TL + 1], ("F%d" % fi,), ("xedge",))
            return f

        def load_inT(srcT, kcn, view, lo, hi, key="inT"):
            for k in range(kcn):
                P.dma("sp", view[:, k, 0:hi - lo], srcT[k * 128:(k + 1) * 128, lo:hi], ("%s%d" % (srcT.name, k),), (key,), "ldin")

        def chunk_rstd(fi):
            si = 2 + uid() % 2
            sqb = Fb[si][:, :].bitcast(BF16)[:, 0:TP]
            act(sqb, Fb[fi][:, :], AF.Square, ("F%d" % fi,), ("F%d" % si,))
            for ti, (a, b) in enumerate(TT):
                mm(PS[ti][:, 0:b - a], ones_bf[:, :], sqb[:, a:b], True, True, ("ones_bf", "F%d" % si), ("ps%d" % ti,))
            colsumsq_finish(TT, 128, list(range(len(TT))))

        def even_post(e):
            for i in range(2):
                ts("dve", pmatg[:, i, :], pmat[:, :], qkgT[:, i * c.NE + e:i * c.NE + e + 1], None, ALU.mult, None, ("pmat", "qkgT"), ("pmatg",))
            for h in range(c.AQ + c.AKV):
                isk = h >= c.AQ
                fi = uid() % 2
                P.dma("sp", Fb[fi][:, :], projT[h * 128:(h + 1) * 128, :], ("projT%d" % h,), ("F%d" % fi,), "ldF%d" % fi)
                chunk_rstd(fi)
                qn = Fb[fi]
                tt("dve", qn[:, :], qn[:, :], rstd[:, :], ALU.mult, ("F%d" % fi, "rstd"), ("F%d" % fi,))
                for ti, (a, b) in enumerate(TT):
                    mm(PS[4 + ti][:, 0:b - a], pmatg[:, 1 if isk else 0, :], qn[:, a:b], True, True, ("pmatg", "F%d" % fi), ("ps%d" % (4 + ti),))
                gi = 6
                for ti, (a, b) in enumerate(TT):
                    tt("dve", Fb[gi][:, a:b], PS[4 + ti][:, 0:b - a], sinT[:, a:b], ALU.mult, ("ps%d" % (4 + ti), "sinT"), ("F6",))
                stt("dve", qn[:, :], qn[:, :], qkgT[:, (1 if isk else 0) * c.NE + e:(1 if isk else 0) * c.NE + e + 1], cosT[:, :],
                    ALU.mult, ALU.mult, ("F%d" % fi, "qkgT", "cosT"), ("F%d" % fi,))
                tt("dve", qn[:, :], qn[:, :], Fb[gi][:, :], ALU.add, ("F%d" % fi, "F6"), ("F%d" % fi,))
                ob = Fb[7][:, :].bitcast(BF16)[:, 0:TP]
                sc_ = 1.0 if isk else 128 ** -0.5
                P.op("act", lambda en, o=ob, i_=qn[:, :], s__=sc_: en.mul(out=o, in_=i_, mul=s__), ("F%d" % fi,), ("F7",))
                P.dma("pool", qT[h * 128:(h + 1) * 128, :], ob, ("F7",), ("qT%d" % h,), "stF7")
                if isk:
                    hk = h - c.AQ
                    P.dma("pool", k_src[hk][:, :], ob[:, c.L0:c.L0 + TL], ("F7",), ("k_src%d" % hk,), "stF7b")
                    P.op("pool", lambda en, a=k_src[hk], b=k_all[hk]: en.collective_compute(
                        "AllGather", ALU.bypass, replica_groups=[list(range(NCORES))], ins=[a[:, :]], outs=[b[:, :]]),
                        ("k_src%d" % hk,), ("k_all",), cc=True)
            vbase = (c.AQ + c.AKV)
            vst = arena[:, 0:2 * VW].rearrange("p (s w) -> p s w", s=2)
            for (ntok, col0, dstl) in ((TL, c.L0, v_src), (TC, c.C0, None)):
                for tb in range(0, ntok, 128):
                    dst = dstl[tb // 128] if dstl is not None else v_ctx
                    vi = uid() % 2
                    memset("dve", vst[:, vi, :], 1.0, ("vst%d" % vi,))
                    for hv in range(c.AKV):
                        fi = uid() % 2
                        P.dma("sp", Fb[fi][:, 0:128], projT[(vbase + hv) * 128:(vbase + hv + 1) * 128, col0 + tb:col0 + tb + 128],
                              ("projT%d" % (vbase + hv),), ("F%d" % fi,), "ldF%d" % fi)
                        pi = uid() % 2
                        tr(PS[pi][:, 0:128], Fb[fi][:, 0:128], ident[:, :], ("F%d" % fi, "ident"), ("ps%d" % pi,))
                        cp("act", vst[:, vi, hv * 129:hv * 129 + 128], PS[pi][:, 0:128], ("ps%d" % pi,), ("vst%d" % vi,))
                    if dstl is None:
                        P.dma("pool", dst[tb:tb + 128, :], vst[:, vi, :], ("vst%d" % vi,), (dst.name,), "stv%d" % vi)
                    else:
                        P.dma("pool", dst[:, :], vst[:, vi, :], ("vst%d" % vi,), (dst.name,), "stv%d" % vi)
                        P.op("pool", lambda en, a=dst, b=v_all[tb // 128]: en.collective_compute(
                            "AllGather", ALU.bypass, replica_groups=[list(range(NCORES))], ins=[a[:, :]], outs=[b[:, :]]),
                            (dst.name,), ("v_all",), cc=True)

        def attention():
            P.barrier(scr[:, 0:1])
            NKL = c.SEQ // 128; NKC = TC // 128; NK = NKL + NKC
            off = 0
            ksb = arena[:, off:off + NK * 128]; off += NK * 128
            vsb = arena[:, off:off + NK * 130].rearrange("p (n w) -> p n w", w=130); off += NK * 130
            qtl = [arena[:, off + i * 512:off + (i + 1) * 512] for i in range(2)]; off += 1024
            ptl = [arena[:, off + i * 512:off + (i + 1) * 512] for i in range(3)]; off += 1536
            ostg = [arena[:, off + i * 128:off + (i + 1) * 128] for i in range(2)]; off += 256
            assert off <= AR_N, (off, AR_N)
            G = c.AQ // c.AKV
            for hk in range(c.AKV):
                P.dma("sp", ksb[:, 0:TC], qT[(c.AQ + hk) * 128:(c.AQ + hk + 1) * 128, c.C0:c.C0 + TC], ("qT%d" % (c.AQ + hk),), ("ksb",), "ldk")
                for i in range(NB):
                    P.dma("sp", ksb[:, TC + i * NCORES * 128:TC + (i + 1) * NCORES * 128].rearrange("p (r t) -> p r t", r=NCORES),
                          k_all[hk].ap().rearrange("(r p) t -> p r t", p=128)[:, :, i * 128:(i + 1) * 128], ("k_all",), ("ksb",), "ldk")
                P.dma("sp", vsb[:, 0:NKC, 0:129], v_ctx.ap().rearrange("(n p) w -> p n w", p=128)[:, :, hk * 129:hk * 129 + 129],
                      ("v_ctx",), ("vsb",), "ldv")
                for i in range(NB):
                    P.dma("sp", vsb[:, NKC + i * NCORES:NKC + (i + 1) * NCORES, 0:129],
                          v_all[i].ap().rearrange("(r p) w -> p r w", p=128)[:, :, hk * 129:hk * 129 + 129], ("v_all",), ("vsb",), "ldv")
                qblocks = [(c.L0 + i * 128, 0, NK) for i in range(TL // 128)] + [(c.C0 + i * 128, 0, NKC) for i in range(TC // 128)]
                for (qc, k0, k1) in qblocks:
                    qi = uid() % 2
                    for j in range(G):
                        hq = hk * G + j
                        P.dma("sp", qtl[qi][:, j * 128:(j + 1) * 128], qT[hq * 128:(hq + 1) * 128, qc:qc + 128], ("qT%d" % hq,), ("qtl%d" % qi,), "ldq%d" % qi)
                    for kc_ in range(k0, k1):
                        si = kc_ % 2; pj = kc_ % 3
                        mm(PS[si][:, 0:G * 128], ksb[:, kc_ * 128:(kc_ + 1) * 128], qtl[qi][:, 0:G * 128], True, True, ("ksb", "qtl%d" % qi), ("ps%d" % si,))
                        act(ptl[pj][:, 0:G * 128], PS[si][:, 0:G * 128], AF.Exp, ("ps%d" % si,), ("ptl%d" % pj,))
                        for j in range(G):
                            mm(PS[2 + j][:, 0:129], ptl[pj][:, j * 128:(j + 1) * 128], vsb[:, kc_, 0:129], kc_ == k0, kc_ == k1 - 1,
                               ("ptl%d" % pj, "vsb"), ("ps%d" % (2 + j),))
                    for j in range(G):
                        hq = hk * G + j
                        P.op("dve", lambda en, o=small[:, j:j + 1], i_=PS[2 + j][:, 128:129]: en.reciprocal(out=o, in_=i_), ("ps%d" % (2 + j),), ("small",))
                        fi = uid() % 2
                        ts("dve", Fb[fi][:, 0:128], PS[2 + j][:, 0:128], small[:, j:j + 1], None, ALU.mult, None, ("ps%d" % (2 + j), "small"), ("F%d" % fi,))
                        tr(PS[6 + j % 2][:, 0:128], Fb[fi][:, 0:128], ident[:, :], ("F%d" % fi, "ident"), ("ps%d" % (6 + j % 2),))
                        oi = uid() % 2
                        cp("act", ostg[oi], PS[6 + j % 2][:, 0:128], ("ps%d" % (6 + j % 2),), ("ostg%d" % oi,))
                        P.dma("pool", mixT[hq * 128:(hq + 1) * 128, qc:qc + 128], ostg[oi], ("ostg%d" % oi,), ("mixT%d" % hq,), "sto%d" % oi)
            P.barrier(scr[:, 0:1])

        def gmlp(e):
            for g in range(c.BG):
                P.dma("sp", Fb[0][:, 0:128], gws[(e * c.BG + g) * 128:(e * c.BG + g + 1) * 128, :], (), ("F0",), "ldF0")
                tr(PS[7][:, 0:128], Fb[0][:, 0:128], ident[:, :], ("F0", "ident"), ("ps7",))
                cp("dve", wsT[:, g, :], PS[7][:, 0:128], ("ps7",), ("wsT",))
            P.dma("sp", bsb[:, :], gbs[e:e + 1, :].broadcast_to([128, c.BG * 128]), (), ("bsb",), "ld_bsb")
            ub = (c.AQ + 2 * c.AKV); vb = ub + c.BG
            vt = [arena[:, i * 128:(i + 1) * 128] for i in range(2)]
            blocks = [c.L0 + i * 128 for i in range(TL // 128)] + [c.C0 + i * 128 for i in range(TC // 128)]
            for g in range(c.BG):
                fi = uid() % 2
                P.dma("sp", Fb[fi][:, :], projT[(vb + g) * 128:(vb + g + 1) * 128, :], ("projT%d" % (vb + g),), ("F%d" % fi,), "ldF%d" % fi)
                act(Fb[fi][:, :], Fb[fi][:, :], AF.Gelu_apprx_tanh, ("F%d" % fi,), ("F%d" % fi,))
                chunk_rstd(fi)
                tt("dve", Fb[fi][:, :], Fb[fi][:, :], rstd[:, :], ALU.mult, ("F%d" % fi, "rstd"), ("F%d" % fi,))
                ts("dve", Fb[fi][:, :], Fb[fi][:, :], gngT[:, e * c.BG + g:e * c.BG + g + 1], None, ALU.mult, None, ("F%d" % fi, "gngT"), ("F%d" % fi,))
                ui = 4 + uid() % 2
                P.dma("sp", Fb[ui][:, :], projT[(ub + g) * 128:(ub + g + 1) * 128, :], ("projT%d" % (ub + g),), ("F%d" % ui,), "ldF%d" % ui)
                act(Fb[ui][:, :], Fb[ui][:, :], AF.Gelu_apprx_tanh, ("F%d" % ui,), ("F%d" % ui,))
                ob = Fb[7][:, :].bitcast(BF16)[:, 0:TP]
                for bc in blocks:
                    pi = uid() % 2
                    tr(PS[pi][:, 0:128], Fb[fi][:, bc:bc + 128], ident[:, :], ("F%d" % fi, "ident"), ("ps%d" % pi,))
                    vi = uid() % 2
                    cp("act", vt[vi], PS[pi][:, 0:128], ("ps%d" % pi,), ("vt%d" % vi,))
                    po = 2 + uid() % 2
                    mm(PS[po][:, 0:128], vt[vi], wsT[:, g, :], True, True, ("vt%d" % vi, "wsT"), ("ps%d" % po,))
                    tt("dve", Fb[6][:, 0:128], PS[po][:, 0:128], bsb[:, g * 128:(g + 1) * 128], ALU.add, ("ps%d" % po, "bsb"), ("F6",))
                    tt("dve", ob[:, bc:bc + 128], Fb[6][:, 0:128], Fb[ui][:, bc:bc + 128], ALU.mult, ("F6", "F%d" % ui), ("F7",))
                row = (c.AQ + g) * 128
                P.dma("pool", mixT[row:row + 128, c.L0:c.L0 + TL], ob[:, c.L0:c.L0 + TL], ("F7",), ("mixT%d" % (c.AQ + g),), "stF7")
                P.dma("pool", mixT[row:row + 128, c.C0:c.C0 + TC], ob[:, c.C0:c.C0 + TC], ("F7",), ("mixT%d" % (c.AQ + g),), "stF7b")

        def conv3(dst, src, w0, w1, w2, bias, rk):
            d = Fb[dst][:, 1:TP - 1]
            if bias is None:
                ts("dve", d, Fb[src][:, 0:TP - 2], w0, None, ALU.mult, None, ("F%d" % src,) + rk, ("F%d" % dst,))
            else:
                ts("dve", d, Fb[src][:, 0:TP - 2], w0, bias, ALU.mult, ALU.add, ("F%d" % src,) + rk, ("F%d" % dst,))
            stt("dve", d, Fb[src][:, 1:TP - 1], w1, d, ALU.mult, ALU.add, ("F%d" % src, "F%d" % dst) + rk, ("F%d" % dst,))
            stt("dve", d, Fb[src][:, 2:TP], w2, d, ALU.mult, ALU.add, ("F%d" % src, "F%d" % dst) + rk, ("F%d" % dst,))

        def odd_conv(o):
            for j in range(c.CG):
                P.dma("sp", Fb[0][:, :], projT[j * 128:(j + 1) * 128, :], ("projT%d" % j,), ("F0",), "ldF0")
                P.dma("sp", Fb[1][:, :], projT[(c.CG + j) * 128:(c.CG + j + 1) * 128, :], ("projT%d" % (c.CG + j),), ("F1",), "ldF1")
                P.dma("sp", Fb[2][:, :], projT[(2 * c.CG + j) * 128:(2 * c.CG + j + 1) * 128, :], ("projT%d" % (2 * c.CG + j),), ("F2",), "ldF2")
                tt("pool", Fb[1][:, :], Fb[1][:, :], Fb[2][:, :], ALU.mult, ("F1", "F2"), ("F1",))
                wv = lambda t_: ccwT[:, (o * 3 + t_) * c.CG + j:(o * 3 + t_) * c.CG + j + 1]
                conv3(3, 1, wv(0), wv(1), wv(2), None, ("ccwT",))
                ob = Fb[7][:, :].bitcast(BF16)[:, 0:TP]
                tt("dve", ob[:, 1:TP - 1], Fb[3][:, 1:TP - 1], Fb[0][:, 1:TP - 1], ALU.mult, ("F3", "F0"), ("F7",))
                P.dma("pool", mixT[j * 128:(j + 1) * 128, 1:TP - 1], ob[:, 1:TP - 1], ("F7",), ("mixT%d" % j,), "stF7")

        def fourier():
            P.barrier(scr[:, 0:1])
            fb0 = 3 * c.CG
            fbf = Fb[6][:, :].bitcast(BF16)[:, 0:TP]
            abst = arena[:, 0:2 * c.DG * 256].rearrange("p (s w) -> p s w", s=2)
            for (ntok, col0, dstl) in ((TL, c.L0, ab_src), (TC, c.C0, None)):
                for g in range(c.DG):
                    fi = uid() % 2
                    P.dma("sp", Fb[fi][:, :], projT[(fb0 + g) * 128:(fb0 + g + 1) * 128, :], ("projT%d" % (fb0 + g),), ("F%d" % fi,), "ldF%d" % fi)
                    fbb = Fb[2 + fi][:, :].bitcast(BF16)[:, 0:TP]
                    cp("act", fbb, Fb[fi][:, :], ("F%d" % fi,), ("F%d" % (2 + fi),))
                    for tb in range(0, ntok, 128):
                        pi = uid() % 2
                        mm(PS[pi][:, 0:256], fbb[:, col0 + tb:col0 + tb + 128], cs_bf[:, :], True, True, ("F%d" % (2 + fi), "cs_bf"), ("ps%d" % pi,))
                        ai = uid() % 2
                        cp("dve", abst[:, ai, 0:256], PS[pi][:, 0:256], ("ps%d" % pi,), ("abst%d" % ai,))
                        if dstl is None:
                            P.dma("pool", ab_ctx[tb:tb + 128, g * 256:(g + 1) * 256], abst[:, ai, 0:256], ("abst%d" % ai,), ("ab_ctx",), "sta%d" % ai)
                        else:
                            P.dma("pool", dstl[tb // 128][g // GH][:, (g % GH) * 256:(g % GH + 1) * 256], abst[:, ai, 0:256],
                                  ("abst%d" % ai,), ("ab_src",), "sta%d" % ai)
            for i in range(NB):
                for gh in range(NGH):
                    P.op("pool", lambda en, a=ab_src[i][gh], b=ab_all[i][gh]: en.collective_compute(
                        "AllGather", ALU.bypass, replica_groups=[list(range(NCORES))], ins=[a[:, :]], outs=[b[:, :]]),
                        ("ab_src",), ("ab_all",), cc=True)
            P.barrier(scr[:, 0:1])
            for (nt, abd, dc, ds_, col0, nk) in ((c.SEQ // 128, None, dftc, dfts, c.L0, TL), (TC // 128, ab_ctx, dftcc, dftcs, c.C0, TC)):
                KT = 128
                off = 0
                dC = arena[:, off:off + nt * KT].rearrange("p (n k) -> p n k", k=KT); off += nt * KT
                dS = arena[:, off:off + nt * KT].rearrange("p (n k) -> p n k", k=KT); off += nt * KT
                abg = [arena[:, off:off + nt * 256].rearrange("p (n w) -> p n w", w=256)] * 2
                off += nt * 256
                assert off <= AR_N, (off, AR_N)
                for k0 in range(0, nk, KT):
                    P.dma("sp", dC, dc.ap().rearrange("(n p) k -> p n k", p=128)[:, :, k0:k0 + KT], (), ("dC",), "lddc")
                    P.dma("sp", dS, ds_.ap().rearrange("(n p) k -> p n k", p=128)[:, :, k0:k0 + KT], (), ("dS",), "ldds")
                    for g in range(c.DG):
                        gi = 0
                        if abd is None:
                            for i in range(NB):
                                P.dma("sp", abg[gi][:, i * NCORES:(i + 1) * NCORES, :],
                                      ab_all[i][g // GH].ap().rearrange("(r p) w -> p r w", p=128)[:, :, (g % GH) * 256:(g % GH + 1) * 256],
                                      ("ab_all",), ("abg%d" % gi,), "ldab%d" % gi)
                        else:
                            P.dma("sp", abg[gi], abd.ap().rearrange("(n p) w -> p n w", p=128)[:, :, g * 256:(g + 1) * 256], ("ab_ctx",), ("abg%d" % gi,), "ldab%d" % gi)
                        pi = uid() % 2
                        for n in range(nt):
                            mm(PS[pi][:, 0:KT], abg[gi][:, n, 0:128], dC[:, n, :], n == 0, False, ("abg%d" % gi, "dC"), ("ps%d" % pi,))
                            mm(PS[pi][:, 0:KT], abg[gi][:, n, 128:256], dS[:, n, :], False, n == nt - 1, ("abg%d" % gi, "dS"), ("ps%d" % pi,))
                        oi = 4 + uid() % 2
                        ob = Fb[oi][:, :].bitcast(BF16)[:, 0:KT]
                        cp("act", ob, PS[pi][:, 0:KT], ("ps%d" % pi,), ("F%d" % oi,))
                        row = (c.CG + g) * 128
                        P.dma("pool", mixT[row:row + 128, col0 + k0:col0 + k0 + KT], ob, ("F%d" % oi,), ("mixT%d" % (c.CG + g),), "stF%d" % oi)
            P.barrier(scr[:, 0:1])

        def ffn_gate(l):
            for j in range(c.FC):
                P.dma("sp", Fb[0][:, :], projT[j * 128:(j + 1) * 128, :], ("projT%d" % j,), ("F0",), "ldF0")
                P.dma("sp", Fb[1][:, :], projT[(c.FC + j) * 128:(c.FC + j + 1) * 128, :], ("projT%d" % (c.FC + j),), ("F1",), "ldF1")
                wv = lambda t_, jj: fcwT[:, (l * 3 + t_) * 2 * c.FC + jj:(l * 3 + t_) * 2 * c.FC + jj + 1]
                bv = lambda jj: fcbT[:, l * 2 * c.FC + jj:l * 2 * c.FC + jj + 1]
                conv3(2, 0, wv(0, j), wv(1, j), wv(2, j), bv(j), ("fcwT", "fcbT"))
                conv3(3, 1, wv(0, c.FC + j), wv(1, c.FC + j), wv(2, c.FC + j), bv(c.FC + j), ("fcwT", "fcbT"))
                act(Fb[2][:, 1:TP - 1], Fb[2][:, 1:TP - 1], AF.Silu, ("F2",), ("F2",))
                ob = Fb[7][:, :].bitcast(BF16)[:, 0:TP]
                tt("pool", ob[:, 1:TP - 1], Fb[2][:, 1:TP - 1], Fb[3][:, 1:TP - 1], ALU.mult, ("F2", "F3"), ("F7",))
                P.dma("pool", hidT[j * 128:(j + 1) * 128, 1:TP - 1], ob[:, 1:TP - 1], ("F7",), ("hidT%d" % j,), "stF7")

        halo_exchange()
        for l in range(L):
            lin_epoch[0] = l
            make_mod(l)
            norm_mod(1)
            if l % 2 == 0:
                e = l // 2
                linear(W_in_a[e], KC, inT, TT, ep_store(projT))
                P.barrier(scr[:, 0:1])
                even_post(e)
                P.barrier(scr[:, 0:1])
                gmlp(e)
                attention()
                wout = W_out_a[e]
            else:
                o = l // 2
                linear(W_in_c[o], KC, inT, TT, ep_store(projT))
                odd_conv(o)
                fourier()
                wout = W_out_c[o]
            P.barrier(scr[:, 0:1])
            load_inT(mixT, KC, inT, 0, TP)
            linear(wout, KC, inT, TT, ep_resid(2))
            P.barrier(scr[:, 0:1])
            halo_exchange()
            norm_mod(2)
            linear(W_up[l], KC, inT, TT, ep_store(projT))
            ffn_gate(l)
            P.barrier(scr[:, 0:1])
            half = (TP + 1) // 2
            for (lo, hi) in ((1, half), (half, TP - 1)):
                hv = arena[:, 0:c.FC * (hi - lo)].rearrange("p (k t) -> p k t", k=c.FC)
                load_inT(hidT, c.FC, hv, lo, hi)
                tl_ = [(a - lo, b - lo) for (a, b) in ctiles(lo, hi)]

                def ep(m, tiles, lo=lo):
                    ep_resid(5)(m, [(pi, a + lo, b + lo) for (pi, a, b) in tiles])
                linear(W_down[l], c.FC, hv, tl_, ep)
                P.barrier(scr[:, 0:1])
            if l < L - 1:
                halo_exchange()

        for tb in range(0, TL, 128):
            for kb in range(0, KC, 2):
                fi = uid() % 2
                for k in range(2):
                    P.dma("sp", Fb[2 + k][:, 0:128], xT[(kb + k) * 128:(kb + k + 1) * 128, c.L0 + tb:c.L0 + tb + 128],
                          ("xT%d" % (kb + k),), ("F%d" % (2 + k),), "ldF%d" % (2 + k))
                    pi = uid() % 2
                    tr(PS[pi][:, 0:128], Fb[2 + k][:, 0:128], ident[:, :], ("F%d" % (2 + k), "ident"), ("ps%d" % pi,))
                    cp("dve" if k % 2 else "act", Fb[fi][:, k * 128:(k + 1) * 128], PS[pi][:, 0:128], ("ps%d" % pi,), ("F%d" % fi,))
                P.dma("pool", y_out[tb:tb + 128, kb * 128:kb * 128 + 256], Fb[fi][:, 0:256], ("F%d" % fi,), ("y",), "stF%d" % fi)
        P.barrier(scr[:, 0:1])
        P.emit(nc)
    return nc


def host_inputs(cfg, inp):
    c = cfg
    D, KC, TL, TC, TP, L = c.D, c.KC, c.TL, c.TC, c.TP, c.L
    f32 = np.float32
    bf = ml_dtypes.bfloat16
    x = np.asarray(inp["x"], f32).reshape(c.SEQ, D)
    ctx = np.ascontiguousarray(np.asarray(inp["ctx"], f32).reshape(TC, D))
    cvec = np.concatenate([np.asarray(inp["c"], f32).reshape(KC, 128), np.asarray(inp["c_ctx"], f32).reshape(KC, 128)], 0)
    rows = c.SEQ // c.GRID_W
    row = np.repeat(np.arange(rows, dtype=f32), c.GRID_W)
    col = np.tile(np.arange(c.GRID_W, dtype=f32), rows)
    half = 64
    inv_freq = (ROPE_THETA ** (-np.arange(0, half, 2, dtype=f32) / half)).astype(f32)
    ang = np.concatenate([row[:, None] * inv_freq, col[:, None] * inv_freq], -1).astype(f32)
    cos = np.repeat(np.cos(ang).astype(f32), 2, axis=1)
    sin = np.repeat(np.sin(ang).astype(f32), 2, axis=1)
    pm = np.zeros((128, 128), f32)
    for i in range(64):
        pm[2 * i, 2 * i + 1] = -1.0
        pm[2 * i + 1, 2 * i] = 1.0
    pmT = np.ascontiguousarray(pm.T)
    cc = np.arange(128)
    angc = 2 * np.pi * np.outer(cc, cc) / 128.0
    csmat = np.concatenate([np.cos(angc), -np.sin(angc)], 1).astype(f32)
    t = np.arange(c.SEQ, dtype=np.int64)
    scl = 1.0 / np.sqrt(c.SEQ * 128.0)
    tcx = np.arange(TC, dtype=np.int64)
    angx = 2 * np.pi * ((np.outer(tcx, tcx)) % TC) / TC
    sclx = 1.0 / np.sqrt(TC * 128.0)
    dft_cc = (np.cos(angx) * sclx).astype(bf); dft_cs = (np.sin(angx) * sclx).astype(bf)
    ident = np.eye(128, dtype=f32)

    def rs(a, n):
        a = np.asarray(a, f32)
        k = a.shape[0] // NCORES
        return [np.ascontiguousarray(a[r * k:(r + 1) * k]) for r in range(NCORES)]
    maps = [dict() for _ in range(NCORES)]
    wada = np.asarray(inp["w_ada"], f32); bada = np.asarray(inp["b_ada"], f32)
    aw = c.ACH * 128
    for r in range(NCORES):
        m = maps[r]
        m["x_loc"] = np.ascontiguousarray(x[r * TL:(r + 1) * TL]); m["ctx"] = ctx; m["cvec"] = cvec
        m["w_ada"] = np.ascontiguousarray(wada[:, :, r * aw:(r + 1) * aw]).reshape(L * D, aw)
        m["b_ada"] = np.ascontiguousarray(bada[:, r * aw:(r + 1) * aw]).reshape(1, L * aw)
        m["norm1_g"] = np.asarray(inp["norm1_g"], f32).reshape(L * KC, 128)
        m["norm2_g"] = np.asarray(inp["norm2_g"], f32).reshape(L * KC, 128)
        m["ffn_conv_w"] = np.asarray(inp["ffn_conv_w"], f32).reshape(-1, 128)
        m["ffn_conv_b"] = np.asarray(inp["ffn_conv_b"], f32).reshape(-1, 128)
        m["conv_w_c"] = np.asarray(inp["conv_w_c"], f32).reshape(-1, 128)
        m["gmlp_norm_g"] = np.asarray(inp["gmlp_norm_g"], f32).reshape(-1, 128)
        m["qk_norm_g"] = np.concatenate([np.asarray(inp["q_norm_g"], f32), np.asarray(inp["k_norm_g"], f32)], 0)
        m["gmlp_w_s"] = np.asarray(inp["gmlp_w_s"], f32).reshape(-1, 128)
        m["gmlp_b_s"] = np.asarray(inp["gmlp_b_s"], f32).reshape(c.NE, -1)
        rc = np.ones((128, TP), f32); rsn = np.zeros((128, TP), f32)
        rc[:, c.L0:c.L0 + TL] = cos[r * TL:(r + 1) * TL].T; rsn[:, c.L0:c.L0 + TL] = sin[r * TL:(r + 1) * TL].T
        m["rope_cos"] = rc; m["rope_sin"] = rsn
        m["csmat"] = csmat
        kk = np.arange(r * TL, (r + 1) * TL, dtype=np.int64)
        tperm = (np.arange(NCORES)[None, :, None] * TL + np.arange(TL // 128)[:, None, None] * 128 + np.arange(128)[None, None, :]).reshape(-1)
        a_ = 2 * np.pi * ((np.outer(t[tperm], kk)) % c.SEQ) / c.SEQ
        m["dft_c"] = (np.cos(a_) * scl).astype(bf); m["dft_s"] = (np.sin(a_) * scl).astype(bf)
        m["dft_cc"] = dft_cc; m["dft_cs"] = dft_cs
        mk = np.zeros((128, 2 * NCORES + 2), f32)
        if r > 0:
            mk[:, r - 1] = 1.0; mk[:, 2 * NCORES] = 1.0
        if r < NCORES - 1:
            mk[:, NCORES + r + 1] = 1.0; mk[:, 2 * NCORES + 1] = 1.0
        m["masks"] = mk; m["ident"] = ident; m["pmat"] = pmT
    for name in ("w_in_a", "w_out_a", "w_in_c", "w_out_c", "w_ffn_up", "w_ffn_down"):
        a = np.asarray(inp[name], f32)
        for i in range(a.shape[0]):
            sh = rs(a[i], NCORES)
            for r in range(NCORES):
                maps[r]["%s%d" % (name, i)] = sh[r]
    return maps


_NC_CACHE = {}


def run(cfg, inp):
    key = (cfg.D, cfg.SEQ, cfg.CTX, cfg.GRID_W)
    if key not in _NC_CACHE:
        _NC_CACHE[key] = build(cfg)
    nc = _NC_CACHE[key]
    maps = host_inputs(cfg, inp)
    res = run_bass_kernel_spmd(nc, maps, core_ids=list(range(NCORES)))
    y = np.concatenate([np.asarray(res.results[r]["y"]) for r in range(NCORES)], 0)
    return y.reshape(1, cfg.SEQ, cfg.D).astype(np.float32)


def kernel(**inputs):
    return run(Cfg(), inputs)
```

```python
from contextlib import ExitStack
import numpy as np
import ml_dtypes
import concourse.bass as bass
import concourse.mybir as mybir
from concourse.bass_utils import run_bass_kernel_spmd

F32 = mybir.dt.float32
BF16 = mybir.dt.bfloat16
AF = mybir.ActivationFunctionType
ALU = mybir.AluOpType
NCORES = 8
EPS = 1e-6
ROPE_THETA = 10000.0


class Cfg:
    def __init__(s, D=4096, SEQ=8192, CTX=256, GRID_W=64, DEPTH=4):
        s.D, s.SEQ, s.CTX, s.GRID_W, s.L = D, SEQ, CTX, GRID_W, DEPTH
        s.KC = D // 128
        NH = D // 128
        s.AQ = NH // 2; s.AKV = s.AQ // 4; s.BG = NH // 2; s.CG = 3 * NH // 4; s.DG = NH // 4
        s.DFF = 3 * D // 2; s.FC = s.DFF // 128
        s.EVEN_IN = (s.AQ + 2 * s.AKV + 2 * s.BG) * 128
        s.ODD_IN = (3 * s.CG + s.DG) * 128
        s.TL = SEQ // NCORES; s.TC = CTX
        s.TP = s.TL + s.TC + 4
        s.L0 = 1; s.RH = s.TL + 1; s.C0 = s.TL + 3
        s.NE = (DEPTH + 1) // 2; s.NO = DEPTH // 2
        s.ACH = 6 * s.KC // NCORES


def ctiles(lo, hi, maxw=512):
    n = -(-(hi - lo) // maxw)
    w = -(-(hi - lo) // n)
    return [(a, min(a + w, hi)) for a in range(lo, hi, w)]


class Op:
    __slots__ = ("eng", "fn", "r", "w", "sem", "incv", "deps", "needs", "count")


class Prog:
    ENG = ("pe", "act", "dve", "pool", "sp")

    def __init__(s):
        s.ops = []
        s.ncc = 0

    def op(s, eng, fn, r=(), w=(), stream=None, cc=False):
        o = Op()
        o.eng, o.fn, o.r, o.w = eng, fn, tuple(r) + ("G",), tuple(w)
        if cc:
            o.sem = "cc"; o.incv = 1; o.w = o.w + ("CCSER",)
        elif stream is not None:
            o.sem = "d_" + stream; o.incv = 16
        else:
            o.sem = eng; o.incv = 1
        o.deps = None; o.needs = bool(stream is not None or cc); o.count = 0
        s.ops.append(o)
        return o

    def dma(s, q, out, in_, r, w, stream, slow=False):
        if slow:
            return s.op(q, lambda e: e.dma_start(out=out, in_=in_, allow_slow_non_contiguous=True), r, w, stream=stream)
        return s.op(q, lambda e: e.dma_start(out=out, in_=in_), r, w, stream=stream)

    def barrier(s, scratch):
        s.op("dve", lambda e: e.memset(scratch, 0.0), r=(), w=("G",))

    def analyze(s):
        lastw = {}
        readers = {}
        for i, o in enumerate(s.ops):
            deps = {}
            def add(j):
                d = s.ops[j]
                if d.sem == "pe" and o.sem == "pe":
                    return
                if d.sem not in deps or deps[d.sem] < j:
                    deps[d.sem] = j
            for k in o.r:
                if k in lastw:
                    add(lastw[k])
            for k in o.w:
                if k in lastw:
                    add(lastw[k])
                for j in readers.get(k, {}).values():
                    if j != i:
                        add(j)
            for k in o.w:
                lastw[k] = i
                readers[k] = {}
            for k in o.r:
                if k not in o.w:
                    readers.setdefault(k, {})[o.sem] = i
            o.deps = list(deps.values())
            for j in o.deps:
                s.ops[j].needs = True
        cum = {}
        for o in s.ops:
            if o.needs:
                cum[o.sem] = cum.get(o.sem, 0) + o.incv
                o.count = cum[o.sem]
        return sorted(cum.keys())

    def emit(s, nc):
        semnames = s.analyze()
        mx = {}
        for o in s.ops:
            mx[o.sem] = max(mx.get(o.sem, 0), o.count)
        print("nsems", len(semnames), "nops", len(s.ops), sorted(mx.items(), key=lambda kv: -kv[1])[:12])
        with ExitStack() as es:
            sems = {n: es.enter_context(nc.semaphore("s_" + n)) for n in semnames}
            blk = es.enter_context(nc.Block())
            per = {e: [] for e in s.ENG}
            for o in s.ops:
                per[o.eng].append(o)

            def run(eng_name):
                def f(e):
                    waited = {}
                    for o in per[eng_name]:
                        for j in o.deps:
                            d = s.ops[j]
                            if waited.get(d.sem, 0) < d.count:
                                e.wait_ge(sems[d.sem], d.count)
                                waited[d.sem] = d.count
                        ins = o.fn(e)
                        if o.needs:
                            ins.then_inc(sems[o.sem], o.incv)
                return f
            blk.tensor(run("pe"))
            blk.scalar(run("act"))
            blk.vector(run("dve"))
            blk.gpsimd(run("pool"))
            blk.sync(run("sp"))


def build(cfg):
    c = cfg
    D, KC, TL, TC, TP, L = c.D, c.KC, c.TL, c.TC, c.TP, c.L
    nc = bass.Bass("TRN2", target_bir_lowering=False)
    P = Prog()

    def din(name, shape, dt=F32):
        return nc.dram_tensor(name, list(shape), dt, kind="ExternalInput")

    def dtmp(name, shape, dt=F32):
        return nc.dram_tensor(name, list(shape), dt)

    x_in = din("x_loc", [TL, D]); ctx_in = din("ctx", [TC, D])
    cvec = din("cvec", [2 * KC, 128])
    wada = din("w_ada", [L * D, c.ACH * 128]); bada = din("b_ada", [1, L * c.ACH * 128])
    n1g = din("norm1_g", [L * KC, 128]); n2g = din("norm2_g", [L * KC, 128])
    fcw = din("ffn_conv_w", [L * 3 * 2 * c.FC, 128]); fcb = din("ffn_conv_b", [L * 2 * c.FC, 128])
    ccw = din("conv_w_c", [c.NO * 3 * c.CG, 128]); gng = din("gmlp_norm_g", [c.NE * c.BG, 128])
    qkg = din("qk_norm_g", [2 * c.NE, 128])
    gws = din("gmlp_w_s", [c.NE * c.BG * 128, 128]); gbs = din("gmlp_b_s", [c.NE, c.BG * 128])
    ropec = din("rope_cos", [128, TP]); ropes = din("rope_sin", [128, TP])
    csmat = din("csmat", [128, 256])
    dftc = din("dft_c", [c.SEQ, TL], BF16); dfts = din("dft_s", [c.SEQ, TL], BF16)
    dftcc = din("dft_cc", [TC, TC], BF16); dftcs = din("dft_cs", [TC, TC], BF16)
    masks = din("masks", [128, 2 * NCORES + 2])
    ident_in = din("ident", [128, 128]); pmat_in = din("pmat", [128, 128])
    GWMAX = 1024

    def wdecl(name, n, K, N):
        out = []
        RB = K // NCORES // 128
        assert RB * 128 * NCORES == K
        for i in range(n):
            src = din("%s%d" % (name, i), [K // NCORES, N])
            groups = []
            for g0 in range(0, N, GWMAX):
                gw = min(GWMAX, N - g0)
                rbs = [(dtmp("%s%d_s%d_%d" % (name, i, g0, rb), [128, gw], BF16),
                        dtmp("%s%d_f%d_%d" % (name, i, g0, rb), [NCORES * 128, gw], BF16)) for rb in range(RB)]
                groups.append((g0, gw, rbs))
            out.append((src, groups, K, N))
        return out
    W_in_a = wdecl("w_in_a", c.NE, D, c.EVEN_IN); W_out_a = wdecl("w_out_a", c.NE, D, D)
    W_in_c = wdecl("w_in_c", c.NO, D, c.ODD_IN); W_out_c = wdecl("w_out_c", c.NO, D, D)
    W_up = wdecl("w_ffn_up", L, D, 2 * c.DFF); W_down = wdecl("w_ffn_down", L, c.DFF, D)
    y_out = nc.dram_tensor("y", [TL, D], F32, kind="ExternalOutput")

    xT = dtmp("xT", [D, TP])
    projT = dtmp("projT", [max(c.EVEN_IN, c.ODD_IN, 2 * c.DFF), TP])
    mixT = dtmp("mixT", [D, TP], BF16)
    hidT = dtmp("hidT", [c.DFF, TP], BF16)
    qT = dtmp("qT", [(c.AQ + c.AKV) * 128, TP], BF16)
    k_src = [dtmp("k_src%d" % h, [128, TL], BF16) for h in range(c.AKV)]
    k_all = [dtmp("k_all%d" % h, [NCORES * 128, TL], BF16) for h in range(c.AKV)]
    NB = TL // 128
    VW = c.AKV * 129
    VW += VW % 2
    v_src = [dtmp("v_src%d" % i, [128, VW], BF16) for i in range(NB)]
    v_all = [dtmp("v_all%d" % i, [NCORES * 128, VW], BF16) for i in range(NB)]
    v_ctx = dtmp("v_ctx", [TC, VW], BF16)
    GH = min(4, c.DG); NGH = c.DG // GH
    ab_src = [[dtmp("ab_src%d_%d" % (i, gh), [128, GH * 256], BF16) for gh in range(NGH)] for i in range(NB)]
    ab_all = [[dtmp("ab_all%d_%d" % (i, gh), [NCORES * 128, GH * 256], BF16) for gh in range(NGH)] for i in range(NB)]
    ab_ctx = dtmp("ab_ctx", [TC, c.DG * 256], BF16)
    ada_src = dtmp("ada_src", [128, L * c.ACH * 2]); ada_all = dtmp("ada_all", [NCORES * 128, L * c.ACH * 2])
    halo_src = dtmp("halo_src", [128, KC * 2]); halo_all = dtmp("halo_all", [NCORES * 128, KC * 2])

    es = ExitStack()
    with es:
        def sb(name, shape, dt=F32):
            return es.enter_context(nc.sbuf_tensor("sb_" + name, list(shape), dt))
        AR_N = max(KC * TP, c.FC * ((TP + 1) // 2 + 1), 4 * 2048 + 2 * 1024)
        arena = sb("arena", [128, AR_N], BF16)
        NF = 8
        Fb = [sb("f%d" % i, [128, TP]) for i in range(NF)]
        KCM = max(KC, c.FC)
        WFL = 256 * KC
        wflat = [sb("wbuf%d" % i, [128, WFL], BF16) for i in range(2)]
        nvec = L * KC
        n1gT = sb("n1gT", [128, nvec]); n2gT = sb("n2gT", [128, nvec])
        fcwT = sb("fcwT", [128, L * 3 * 2 * c.FC]); fcbT = sb("fcbT", [128, L * 2 * c.FC])
        ccwT = sb("ccwT", [128, c.NO * 3 * c.CG]); gngT = sb("gngT", [128, c.NE * c.BG])
        qkgT = sb("qkgT", [128, 2 * c.NE]); sT = sb("sT", [128, KC, 2])
        adaF = sb("adaF", [128, L, 6 * KC, 2])
        mod = sb("mod", [128, 6, 2, KC])
        ident = sb("ident", [128, 128]); pmatg = sb("pmatg", [128, 2, 128]); pmat = sb("pmat", [128, 128])
        ones_bf = sb("ones_bf", [128, 128], BF16); ones_f = sb("ones_f", [128, 128])
        cosT = sb("cosT", [128, TP]); sinT = sb("sinT", [128, TP])
        cs_bf = sb("cs_bf", [128, 256], BF16)
        wsT = sb("wsT", [128, c.BG, 128], BF16); bsb = sb("bsb", [128, c.BG * 128])
        msk = sb("msk", [128, 2 * NCORES + 2])
        xhalo = sb("xhalo", [128, KC, 2]); xedge = sb("xedge", [128, KC, 2]); hall = sb("hall", [128, NCORES, KC, 2])
        rstd = sb("rstd", [128, TP])
        small = sb("small", [128, 512])
        scr = sb("scr", [128, 8])
        PS = [es.enter_context(nc.psum_tensor("ps%d" % i, [128, 512], F32)) for i in range(8)]

        def act(out, in_, func, r, w, bias=None, scale=None):
            kw = {}
            if bias is not None: kw["bias"] = bias
            if scale is not None: kw["scale"] = scale
            P.op("act", lambda e: e.activation(out=out, in_=in_, func=func, **kw), r, w)

        def tt(eng, out, a, b, op, r, w):
            P.op(eng, lambda e: e.tensor_tensor(out=out, in0=a, in1=b, op=op), r, w)

        def ts(eng, out, a, s1, s2, op0, op1, r, w):
            if op1 is None:
                P.op(eng, lambda e: e.tensor_scalar(out=out, in0=a, scalar1=s1, scalar2=None, op0=op0), r, w)
            else:
                P.op(eng, lambda e: e.tensor_scalar(out=out, in0=a, scalar1=s1, scalar2=s2, op0=op0, op1=op1), r, w)

        def stt(eng, out, a, sc, b, op0, op1, r, w):
            P.op(eng, lambda e: e.scalar_tensor_tensor(out=out, in0=a, scalar=sc, in1=b, op0=op0, op1=op1), r, w)

        def cp(eng, out, in_, r, w):
            if eng == "act":
                P.op("act", lambda e: e.copy(out=out, in_=in_), r, w)
            else:
                P.op(eng, lambda e: e.tensor_copy(out=out, in_=in_), r, w)

        def mm(out, lhsT, rhs, start, stop, r, w):
            P.op("pe", lambda e: e.matmul(out, lhsT, rhs, start=start, stop=stop), r, w)

        def tr(out, in_, idn, r, w):
            P.op("pe", lambda e: e.transpose(out, in_, idn), r, w)

        def memset(eng, ap, v, w):
            P.op(eng, lambda e: e.memset(ap, v), (), w)

        cnt = [0]

        def uid():
            cnt[0] += 1
            return cnt[0]

        def ld(dst, src, key):
            P.dma("sp", dst, src, r=(), w=(key,), stream="c_" + key)
        ld(ident[:, :], ident_in[:, :], "ident"); ld(pmat[:, :], pmat_in[:, :], "pmat")
        ld(cosT[:, :], ropec[:, :], "cosT"); ld(sinT[:, :], ropes[:, :], "sinT")
        ld(msk[:, :], masks[:, :], "msk"); ld(Fb[0][:, 0:256], csmat[:, :], "F0")
        cp("dve", cs_bf[:, :], Fb[0][:, 0:256], ("F0",), ("cs_bf",))
        memset("dve", ones_f[:, :], 1.0, ("ones_f",)); memset("dve", ones_bf[:, :], 1.0, ("ones_bf",))
        memset("dve", xhalo[:, :, :], 0.0, ("xhalo",))

        def vecT(dst, dst_key, src, R):
            for r0 in range(0, R, 128):
                rn = min(128, R - r0)
                P.dma("sp", Fb[1][0:rn, 0:128], src[r0:r0 + rn, :], r=(), w=("F1",), stream="c1")
                tr(PS[7][:, 0:rn], Fb[1][0:rn, 0:128], ident[0:rn, 0:rn], ("F1", "ident"), ("ps7",))
                cp("dve", dst[:, r0:r0 + rn], PS[7][:, 0:rn], ("ps7",), (dst_key,))
        vecT(n1gT, "n1gT", n1g, L * KC); vecT(n2gT, "n2gT", n2g, L * KC)
        vecT(fcwT, "fcwT", fcw, L * 6 * c.FC); vecT(fcbT, "fcbT", fcb, L * 2 * c.FC)
        vecT(ccwT, "ccwT", ccw, c.NO * 3 * c.CG); vecT(gngT, "gngT", gng, c.NE * c.BG)
        vecT(qkgT, "qkgT", qkg, 2 * c.NE)
        sraw = Fb[2]
        vecT(sraw, "F2", cvec, 2 * KC)
        for s_ in range(2):
            act(sT[:, :, s_], sraw[:, s_ * KC:(s_ + 1) * KC], AF.Silu, ("F2",), ("sT",))

        adast = Fb[3]
        wada_v = wada.ap().rearrange("(l kc p) n -> l p kc n", l=L, p=128)
        wa = [arena[:, i * KC * 256:(i + 1) * KC * 256].bitcast(F32).rearrange("p (k n) -> p k n", k=KC) for i in range(2)]
        it = 0
        for l in range(L):
            for jj in range(c.ACH):
                slot = it % 2; it += 1
                P.dma("sp", wa[slot], wada_v[l][:, :, jj * 128:(jj + 1) * 128], (), ("wa%d" % slot,), "wa%d" % slot)
                po = PS[6][:, 2 * (it % 64):2 * (it % 64) + 2]
                for kc in range(KC):
                    mm(po, wa[slot][:, kc, :], sT[:, kc, :], kc == 0, False, ("wa%d" % slot, "sT"), ("ps6",))
                off = (l * c.ACH + jj) * 128
                bsl = small[0:1, slot * 128:(slot + 1) * 128]
                P.dma("sp", bsl, bada[0:1, off:off + 128], (), ("bsl%d" % slot,), "bsl%d" % slot)
                mm(po, bsl, ones_f[0:1, 0:2], False, True, ("bsl%d" % slot, "ones_f"), ("ps6",))
                cp("dve", adast[:, (l * c.ACH + jj) * 2:(l * c.ACH + jj) * 2 + 2], po, ("ps6",), ("F3",))
        NA = L * c.ACH * 2
        P.dma("pool", ada_src[:, :], adast[:, 0:NA], ("F3",), ("ada_src",), "st0")
        P.op("pool", lambda e: e.collective_compute("AllGather", ALU.bypass, replica_groups=[list(range(NCORES))],
                                                    ins=[ada_src[:, :]], outs=[ada_all[:, :]]), ("ada_src",), ("ada_all",), cc=True)
        for r_ in range(NCORES):
            P.dma("sp", adaF[:, :, r_ * c.ACH:(r_ + 1) * c.ACH, :].rearrange("p l j s -> p l (j s)"),
                  ada_all[r_ * 128:(r_ + 1) * 128, :].rearrange("p (l n) -> p l n", l=L), ("ada_all",), ("adaF",), "ld_ada")
        P.barrier(scr[:, 0:1])

        CB = min(1024, TP)
        castb = [arena[:, i * CB:(i + 1) * CB] for i in range(4)]
        cast_i = [0]

        def prep_weight(went):
            src, groups, K, N = went
            for (g0, gw, rbs) in groups:
                for rb, (wsrc, wfull) in enumerate(rbs):
                    for c0 in range(0, gw, CB):
                        cw = min(CB, gw - c0)
                        i = cast_i[0]; cast_i[0] += 1
                        fs = 6 + (i % 2); cs = i % 4
                        P.dma("sp", Fb[fs][:, 0:cw], src[rb * 128:(rb + 1) * 128, g0 + c0:g0 + c0 + cw], (), ("F%d" % fs,), "ldF%d" % fs)
                        eng = ("act", "dve")[i % 2]
                        cp(eng, castb[cs][:, 0:cw], Fb[fs][:, 0:cw], ("F%d" % fs,), ("cast%d" % cs,))
                        P.dma("pool", wsrc[:, c0:c0 + cw], castb[cs][:, 0:cw], ("cast%d" % cs,), (wsrc.name,), "stc%d" % cs)
                    P.op("pool", lambda e, a=wsrc, b=wfull: e.collective_compute(
                        "AllGather", ALU.bypass, replica_groups=[list(range(NCORES))], ins=[a[:, :]], outs=[b[:, :]]),
                        (wsrc.name,), (wfull.name,), cc=True)

        for lst in (W_in_a, W_out_a, W_in_c, W_out_c, W_up, W_down):
            for went in lst:
                prep_weight(went)

        def load_xT(src, ntok, col0):
            for tb in range(0, ntok, 128):
                tn = min(128, ntok - tb)
                for kb in range(0, KC, 2):
                    P.dma("sp", Fb[0][0:tn, 0:256], src[tb:tb + tn, kb * 128:kb * 128 + 256], (), ("F0",), "ldF0")
                    for k in range(2):
                        pi = uid() % 2
                        tr(PS[pi][:, 0:tn], Fb[0][0:tn, k * 128:(k + 1) * 128], ident[0:tn, 0:tn], ("F0", "ident"), ("ps%d" % pi,))
                        si = 1 + uid() % 2
                        cp("dve" if k % 2 else "act", Fb[si][:, 0:tn], PS[pi][:, 0:tn], ("ps%d" % pi,), ("F%d" % si,))
                        P.dma("pool", xT[(kb + k) * 128:(kb + k + 1) * 128, col0 + tb:col0 + tb + tn], Fb[si][:, 0:tn],
                              ("F%d" % si,), ("xT%d" % (kb + k),), "stF%d" % si)
                        if src is x_in:
                            if tb == 0:
                                cp("dve", xedge[:, kb + k, 0:1], Fb[si][:, 0:1], ("F%d" % si,), ("xedge",))
                            if tb + tn == ntok:
                                cp("dve", xedge[:, kb + k, 1:2], Fb[si][:, tn - 1:tn], ("F%d" % si,), ("xedge",))
        memset("dve", Fb[3][:, :], 0.0, ("F3",))
        for k in range(KC):
            P.dma("pool", xT[k * 128:(k + 1) * 128, :], Fb[3][:, :], ("F3",), ("xT%d" % k,), "stz")
        P.barrier(scr[:, 0:1])
        load_xT(x_in, TL, c.L0)
        load_xT(ctx_in, TC, c.C0)
        P.barrier(scr[:, 0:1])

        def halo_exchange():
            P.dma("pool", halo_src[:, :], xedge[:, :, :].rearrange("p k j -> p (k j)"), ("xedge",), ("halo_src",), "st0")
            P.op("pool", lambda e: e.collective_compute("AllGather", ALU.bypass, replica_groups=[list(range(NCORES))],
                                                        ins=[halo_src[:, :]], outs=[halo_all[:, :]]), ("halo_src",), ("halo_all",), cc=True)
            P.dma("sp", hall[:, :, :, :].rearrange("p r k j -> p r (k j)"), halo_all.ap().rearrange("(r p) n -> p r n", p=128),
                  ("halo_all",), ("hall",), "ld_hall")
            for side, j in ((0, 1), (1, 0)):
                for r_ in range(NCORES):
                    mcol = msk[:, side * NCORES + r_:side * NCORES + r_ + 1]
                    if r_ == 0:
                        ts("dve", xhalo[:, :, side], hall[:, r_, :, j], mcol, None, ALU.mult, None, ("hall", "msk"), ("xhalo",))
                    else:
                        stt("dve", xhalo[:, :, side], hall[:, r_, :, j], mcol, xhalo[:, :, side], ALU.mult, ALU.add,
                            ("hall", "msk", "xhalo"), ("xhalo",))

        def make_mod(l):
            for s_ in range(2):
                sec = lambda i: adaF[:, l, i * KC:(i + 1) * KC, s_]
                stt("dve", mod[:, 0, s_, :], sec(1), 1.0, n1gT[:, l * KC:(l + 1) * KC], ALU.add, ALU.mult, ("adaF", "n1gT"), ("mod",))
                cp("dve", mod[:, 1, s_, :], sec(0), ("adaF",), ("mod",))
                cp("dve", mod[:, 2, s_, :], sec(2), ("adaF",), ("mod",))
                stt("dve", mod[:, 3, s_, :], sec(4), 1.0, n2gT[:, l * KC:(l + 1) * KC], ALU.add, ALU.mult, ("adaF", "n2gT"), ("mod",))
                cp("dve", mod[:, 4, s_, :], sec(3), ("adaF",), ("mod",))
                cp("dve", mod[:, 5, s_, :], sec(5), ("adaF",), ("mod",))

        inT = arena[:, 0:KC * TP].rearrange("p (k t) -> p k t", k=KC)
        TT = ctiles(0, TP)

        def colsumsq_finish(ncols_tiles, nfeat, ps_ids):
            for ti, (a, b) in enumerate(ncols_tiles):
                act(rstd[:, a:b], PS[ps_ids[ti]][:, 0:b - a], AF.Sqrt, ("ps%d" % ps_ids[ti],), ("rstd",), bias=EPS_T[:, 0:1], scale=1.0 / nfeat)
            P.op("dve", lambda e: e.reciprocal(out=rstd[:, :], in_=rstd[:, :]), ("rstd",), ("rstd",))

        EPS_T = sb("eps_t", [128, 1])
        memset("dve", EPS_T[:, :], EPS, ("eps_t",))

        def norm_mod(which):
            ai, bi = (0, 1) if which == 1 else (3, 4)
            for k in range(KC):
                fi = uid() % 2
                P.dma("sp", Fb[fi][:, :], xT[k * 128:(k + 1) * 128, :], ("xT%d" % k,), ("F%d" % fi,), "ldF%d" % fi)
                cp("dve", Fb[fi][:, 0:1], xhalo[:, k, 0:1], ("xhalo",), ("F%d" % fi,))
                cp("dve", Fb[fi][:, c.RH:c.RH + 1], xhalo[:, k, 1:2], ("xhalo",), ("F%d" % fi,))
                sq = arena[:, KC * TP - TP:KC * TP] if False else None
                si = 2 + uid() % 2
                sqb = Fb[si][:, :].bitcast(BF16)[:, 0:TP]
                act(sqb, Fb[fi][:, :], AF.Square, ("F%d" % fi,), ("F%d" % si,))
                for ti, (a, b) in enumerate(TT):
                    mm(PS[ti][:, 0:b - a], ones_bf[:, :], sqb[:, a:b], k == 0, k == KC - 1, ("ones_bf", "F%d" % si), ("ps%d" % ti,))
            colsumsq_finish(TT, D, list(range(len(TT))))
            for k in range(KC):
                fi = uid() % 2
                P.dma("sp", Fb[fi][:, :], xT[k * 128:(k + 1) * 128, :], ("xT%d" % k,), ("F%d" % fi,), "ldF%d" % fi)
                cp("dve", Fb[fi][:, 0:1], xhalo[:, k, 0:1], ("xhalo",), ("F%d" % fi,))
                cp("dve", Fb[fi][:, c.RH:c.RH + 1], xhalo[:, k, 1:2], ("xhalo",), ("F%d" % fi,))
                tt("dve", Fb[fi][:, :], Fb[fi][:, :], rstd[:, :], ALU.mult, ("F%d" % fi, "rstd"), ("F%d" % fi,))
                act(inT[:, k, 0:c.RH + 1], Fb[fi][:, 0:c.RH + 1], AF.Identity, ("F%d" % fi, "mod"), ("inT",),
                    bias=mod[:, bi, 0, k:k + 1], scale=mod[:, ai, 0, k:k + 1])
                act(inT[:, k, c.C0:c.C0 + TC], Fb[fi][:, c.C0:c.C0 + TC], AF.Identity, ("F%d" % fi, "mod"), ("inT",),
                    bias=mod[:, bi, 1, k:k + 1], scale=mod[:, ai, 1, k:k + 1])
            memset("dve", inT[:, :, c.RH + 1:c.C0], 0.0, ("inT",))
            memset("dve", inT[:, :, TP - 1:TP], 0.0, ("inT",))
            ts("dve", inT[:, :, 0:1], inT[:, :, 0:1], msk[:, 2 * NCORES:2 * NCORES + 1], None, ALU.mult, None, ("inT", "msk"), ("inT",))
            ts("dve", inT[:, :, c.RH:c.RH + 1], inT[:, :, c.RH:c.RH + 1], msk[:, 2 * NCORES + 1:2 * NCORES + 2], None, ALU.mult, None, ("inT", "msk"), ("inT",))

        lin_it = [0]
        lin_ps = [0]
        lin_epoch = [0]

        def linear(went, kcn, inv, col_tiles, epilogue, in_key="inT"):
            src, groups, K, N = went
            RB = kcn // NCORES
            MW = 256 if kcn * 256 <= WFL else 128
            wbuf = [wflat[i][:, 0:kcn * MW].rearrange("p (k m) -> p k m", k=kcn) for i in range(2)]
            for (g0, gw, rbs) in groups:
                for t0 in range(0, gw, MW):
                    tw = min(MW, gw - t0)
                    slot = lin_it[0] % 2; lin_it[0] += 1
                    for rb, (wsrc, wfull) in enumerate(rbs):
                        P.dma("sp", wbuf[slot][:, rb * NCORES:(rb + 1) * NCORES, 0:tw],
                              wfull.ap().rearrange("(r p) n -> p r n", p=128)[:, :, t0:t0 + tw], (wfull.name,), ("wbuf%d" % slot,),
                              "wb%d_%d" % (slot, lin_epoch[0]))
                    for m0 in range(0, tw, 128):
                        m = (g0 + t0 + m0) // 128
                        pbase = 0 if lin_ps[0] % 2 == 0 else 4
                        lin_ps[0] += 1
                        for ti, (a, b) in enumerate(col_tiles):
                            pi = pbase + ti
                            for j in range(kcn):
                                kc_ = (j % NCORES) * RB + j // NCORES
                                mm(PS[pi][:, 0:b - a], wbuf[slot][:, j, m0:m0 + 128], inv[:, kc_, a:b], j == 0, j == kcn - 1,
                                   ("wbuf%d" % slot, in_key), ("ps%d" % pi,))
                        epilogue(m, [(pbase + ti, a, b) for ti, (a, b) in enumerate(col_tiles)])

        def ep_store(dst):
            def f(m, tiles):
                fi = 4 + uid() % 2
                for j, (pi, a, b) in enumerate(tiles):
                    cp("act" if j % 2 == 0 else "dve", Fb[fi][:, a:b], PS[pi][:, 0:b - a], ("ps%d" % pi,), ("F%d" % fi,))
                lo, hi = tiles[0][1], tiles[-1][2]
                P.dma("pool", dst[m * 128:(m + 1) * 128, lo:hi], Fb[fi][:, lo:hi], ("F%d" % fi,), ("%s%d" % (dst.name, m),), "stF%d" % fi)
            return f

        def ep_resid(gidx, final=False):
            def f(m, tiles):
                fi = 4 + uid() % 2
                lo, hi = tiles[0][1], tiles[-1][2]
                P.dma("sp", Fb[fi][:, lo:hi], xT[m * 128:(m + 1) * 128, lo:hi], ("xT%d" % m,), ("F%d" % fi,), "ldF%d" % fi)
                for (pi, a, b) in tiles:
                    for (ra, rb, s_) in ((0, c.RH + 1, 0), (c.RH + 1, TP, 1)):
                        aa, bb = max(a, ra), min(b, rb)
                        if aa < bb:
                            stt("dve", Fb[fi][:, aa:bb], PS[pi][:, aa - a:bb - a], mod[:, gidx, s_, m:m + 1], Fb[fi][:, aa:bb],
                                ALU.mult, ALU.add, ("ps%d" % pi, "mod", "F%d" % fi), ("F%d" % fi,))
                P.dma("pool", xT[m * 128:(m + 1) * 128, lo:hi], Fb[fi][:, lo:hi], ("F%d" % fi,), ("xT%d" % m,), "stF%d" % fi)
                if lo <= c.L0 < hi:
                    cp("dve", xedge[:, m, 0:1], Fb[fi][:, c.L0:c.L0 + 1], ("F%d" % fi,), ("xedge",))
                if lo <= TL < hi:
                    cp("dve", xedge[:, m, 1:2], Fb[fi][:, TL:TL + 1], ("F%d" % fi,), ("xedge",))
            return f

        def load_inT(srcT, kcn, view, lo, hi, key="inT"):
            for k in range(kcn):
                P.dma("sp", view[:, k, 0:hi - lo], srcT[k * 128:(k + 1) * 128, lo:hi], ("%s%d" % (srcT.name, k),), (key,), "ldin")

        def chunk_rstd(fi):
            si = 2 + uid() % 2
            sqb = Fb[si][:, :].bitcast(BF16)[:, 0:TP]
            act(sqb, Fb[fi][:, :], AF.Square, ("F%d" % fi,), ("F%d" % si,))
            for ti, (a, b) in enumerate(TT):
                mm(PS[ti][:, 0:b - a], ones_bf[:, :], sqb[:, a:b], True, True, ("ones_bf", "F%d" % si), ("ps%d" % ti,))
            colsumsq_finish(TT, 128, list(range(len(TT))))

        def even_post(e):
            for i in range(2):
                ts("dve", pmatg[:, i, :], pmat[:, :], qkgT[:, i * c.NE + e:i * c.NE + e + 1], None, ALU.mult, None, ("pmat", "qkgT"), ("pmatg",))
            for h in range(c.AQ + c.AKV):
                isk = h >= c.AQ
                fi = uid() % 2
                P.dma("sp", Fb[fi][:, :], projT[h * 128:(h + 1) * 128, :], ("projT%d" % h,), ("F%d" % fi,), "ldF%d" % fi)
                chunk_rstd(fi)
                qn = Fb[fi]
                tt("dve", qn[:, :], qn[:, :], rstd[:, :], ALU.mult, ("F%d" % fi, "rstd"), ("F%d" % fi,))
                for ti, (a, b) in enumerate(TT):
                    mm(PS[4 + ti][:, 0:b - a], pmatg[:, 1 if isk else 0, :], qn[:, a:b], True, True, ("pmatg", "F%d" % fi), ("ps%d" % (4 + ti),))
                gi = 6
                for ti, (a, b) in enumerate(TT):
                    tt("dve", Fb[gi][:, a:b], PS[4 + ti][:, 0:b - a], sinT[:, a:b], ALU.mult, ("ps%d" % (4 + ti), "sinT"), ("F6",))
                stt("dve", qn[:, :], qn[:, :], qkgT[:, (1 if isk else 0) * c.NE + e:(1 if isk else 0) * c.NE + e + 1], cosT[:, :],
                    ALU.mult, ALU.mult, ("F%d" % fi, "qkgT", "cosT"), ("F%d" % fi,))
                tt("dve", qn[:, :], qn[:, :], Fb[gi][:, :], ALU.add, ("F%d" % fi, "F6"), ("F%d" % fi,))
                ob = Fb[7][:, :].bitcast(BF16)[:, 0:TP]
                sc_ = 1.0 if isk else 128 ** -0.5
                P.op("act", lambda en, o=ob, i_=qn[:, :], s__=sc_: en.mul(out=o, in_=i_, mul=s__), ("F%d" % fi,), ("F7",))
                P.dma("pool", qT[h * 128:(h + 1) * 128, :], ob, ("F7",), ("qT%d" % h,), "stF7")
                if isk:
                    hk = h - c.AQ
                    P.dma("pool", k_src[hk][:, :], ob[:, c.L0:c.L0 + TL], ("F7",), ("k_src%d" % hk,), "stF7b")
                    P.op("pool", lambda en, a=k_src[hk], b=k_all[hk]: en.collective_compute(
                        "AllGather", ALU.bypass, replica_groups=[list(range(NCORES))], ins=[a[:, :]], outs=[b[:, :]]),
                        ("k_src%d" % hk,), ("k_all",), cc=True)
            vbase = (c.AQ + c.AKV)
            vst = arena[:, 0:2 * VW].rearrange("p (s w) -> p s w", s=2)
            for (ntok, col0, dstl) in ((TL, c.L0, v_src), (TC, c.C0, None)):
                for tb in range(0, ntok, 128):
                    dst = dstl[tb // 128] if dstl is not None else v_ctx
                    vi = uid() % 2
                    memset("dve", vst[:, vi, :], 1.0, ("vst%d" % vi,))
                    for hv in range(c.AKV):
                        fi = uid() % 2
                        P.dma("sp", Fb[fi][:, 0:128], projT[(vbase + hv) * 128:(vbase + hv + 1) * 128, col0 + tb:col0 + tb + 128],
                              ("projT%d" % (vbase + hv),), ("F%d" % fi,), "ldF%d" % fi)
                        pi = uid() % 2
                        tr(PS[pi][:, 0:128], Fb[fi][:, 0:128], ident[:, :], ("F%d" % fi, "ident"), ("ps%d" % pi,))
                        cp("act", vst[:, vi, hv * 129:hv * 129 + 128], PS[pi][:, 0:128], ("ps%d" % pi,), ("vst%d" % vi,))
                    if dstl is None:
                        P.dma("pool", dst[tb:tb + 128, :], vst[:, vi, :], ("vst%d" % vi,), (dst.name,), "stv%d" % vi)
                    else:
                        P.dma("pool", dst[:, :], vst[:, vi, :], ("vst%d" % vi,), (dst.name,), "stv%d" % vi)
                        P.op("pool", lambda en, a=dst, b=v_all[tb // 128]: en.collective_compute(
                            "AllGather", ALU.bypass, replica_groups=[list(range(NCORES))], ins=[a[:, :]], outs=[b[:, :]]),
                            (dst.name,), ("v_all",), cc=True)

        def attention():
            P.barrier(scr[:, 0:1])
            NKL = c.SEQ // 128; NKC = TC // 128; NK = NKL + NKC
            off = 0
            ksb = arena[:, off:off + NK * 128]; off += NK * 128
            vsb = arena[:, off:off + NK * 130].rearrange("p (n w) -> p n w", w=130); off += NK * 130
            qtl = [arena[:, off + i * 512:off + (i + 1) * 512] for i in range(2)]; off += 1024
            ptl = [arena[:, off + i * 512:off + (i + 1) * 512] for i in range(3)]; off += 1536
            ostg = [arena[:, off + i * 128:off + (i + 1) * 128] for i in range(2)]; off += 256
            assert off <= AR_N, (off, AR_N)
            G = c.AQ // c.AKV
            for hk in range(c.AKV):
                P.dma("sp", ksb[:, 0:TC], qT[(c.AQ + hk) * 128:(c.AQ + hk + 1) * 128, c.C0:c.C0 + TC], ("qT%d" % (c.AQ + hk),), ("ksb",), "ldk")
                for i in range(NB):
                    P.dma("sp", ksb[:, TC + i * NCORES * 128:TC + (i + 1) * NCORES * 128].rearrange("p (r t) -> p r t", r=NCORES),
                          k_all[hk].ap().rearrange("(r p) t -> p r t", p=128)[:, :, i * 128:(i + 1) * 128], ("k_all",), ("ksb",), "ldk")
                P.dma("sp", vsb[:, 0:NKC, 0:129], v_ctx.ap().rearrange("(n p) w -> p n w", p=128)[:, :, hk * 129:hk * 129 + 129],
                      ("v_ctx",), ("vsb",), "ldv")
                for i in range(NB):
                    P.dma("sp", vsb[:, NKC + i * NCORES:NKC + (i + 1) * NCORES, 0:129],
                          v_all[i].ap().rearrange("(r p) w -> p r w", p=128)[:, :, hk * 129:hk * 129 + 129], ("v_all",), ("vsb",), "ldv")
                qblocks = [(c.L0 + i * 128, 0, NK) for i in range(TL // 128)] + [(c.C0 + i * 128, 0, NKC) for i in range(TC // 128)]
                for (qc, k0, k1) in qblocks:
                    qi = uid() % 2
                    for j in range(G):
                        hq = hk * G + j
                        P.dma("sp", qtl[qi][:, j * 128:(j + 1) * 128], qT[hq * 128:(hq + 1) * 128, qc:qc + 128], ("qT%d" % hq,), ("qtl%d" % qi,), "ldq%d" % qi)
                    for kc_ in range(k0, k1):
                        si = kc_ % 2; pj = kc_ % 3
                        mm(PS[si][:, 0:G * 128], ksb[:, kc_ * 128:(kc_ + 1) * 128], qtl[qi][:, 0:G * 128], True, True, ("ksb", "qtl%d" % qi), ("ps%d" % si,))
                        act(ptl[pj][:, 0:G * 128], PS[si][:, 0:G * 128], AF.Exp, ("ps%d" % si,), ("ptl%d" % pj,))
                        for j in range(G):
                            mm(PS[2 + j][:, 0:129], ptl[pj][:, j * 128:(j + 1) * 128], vsb[:, kc_, 0:129], kc_ == k0, kc_ == k1 - 1,
                               ("ptl%d" % pj, "vsb"), ("ps%d" % (2 + j),))
                    for j in range(G):
                        hq = hk * G + j
                        P.op("dve", lambda en, o=small[:, j:j + 1], i_=PS[2 + j][:, 128:129]: en.reciprocal(out=o, in_=i_), ("ps%d" % (2 + j),), ("small",))
                        fi = uid() % 2
                        ts("dve", Fb[fi][:, 0:128], PS[2 + j][:, 0:128], small[:, j:j + 1], None, ALU.mult, None, ("ps%d" % (2 + j), "small"), ("F%d" % fi,))
                        tr(PS[6 + j % 2][:, 0:128], Fb[fi][:, 0:128], ident[:, :], ("F%d" % fi, "ident"), ("ps%d" % (6 + j % 2),))
                        oi = uid() % 2
                        cp("act", ostg[oi], PS[6 + j % 2][:, 0:128], ("ps%d" % (6 + j % 2),), ("ostg%d" % oi,))
                        P.dma("pool", mixT[hq * 128:(hq + 1) * 128, qc:qc + 128], ostg[oi], ("ostg%d" % oi,), ("mixT%d" % hq,), "sto%d" % oi)
            P.barrier(scr[:, 0:1])

        def gmlp(e):
            for g in range(c.BG):
                P.dma("sp", Fb[0][:, 0:128], gws[(e * c.BG + g) * 128:(e * c.BG + g + 1) * 128, :], (), ("F0",), "ldF0")
                tr(PS[7][:, 0:128], Fb[0][:, 0:128], ident[:, :], ("F0", "ident"), ("ps7",))
                cp("dve", wsT[:, g, :], PS[7][:, 0:128], ("ps7",), ("wsT",))
            P.dma("sp", bsb[:, :], gbs[e:e + 1, :].broadcast_to([128, c.BG * 128]), (), ("bsb",), "ld_bsb")
            ub = (c.AQ + 2 * c.AKV); vb = ub + c.BG
            vt = [arena[:, i * 128:(i + 1) * 128] for i in range(2)]
            blocks = [c.L0 + i * 128 for i in range(TL // 128)] + [c.C0 + i * 128 for i in range(TC // 128)]
            for g in range(c.BG):
                fi = uid() % 2
                P.dma("sp", Fb[fi][:, :], projT[(vb + g) * 128:(vb + g + 1) * 128, :], ("projT%d" % (vb + g),), ("F%d" % fi,), "ldF%d" % fi)
                act(Fb[fi][:, :], Fb[fi][:, :], AF.Gelu_apprx_tanh, ("F%d" % fi,), ("F%d" % fi,))
                chunk_rstd(fi)
                tt("dve", Fb[fi][:, :], Fb[fi][:, :], rstd[:, :], ALU.mult, ("F%d" % fi, "rstd"), ("F%d" % fi,))
                ts("dve", Fb[fi][:, :], Fb[fi][:, :], gngT[:, e * c.BG + g:e * c.BG + g + 1], None, ALU.mult, None, ("F%d" % fi, "gngT"), ("F%d" % fi,))
                ui = 4 + uid() % 2
                P.dma("sp", Fb[ui][:, :], projT[(ub + g) * 128:(ub + g + 1) * 128, :], ("projT%d" % (ub + g),), ("F%d" % ui,), "ldF%d" % ui)
                act(Fb[ui][:, :], Fb[ui][:, :], AF.Gelu_apprx_tanh, ("F%d" % ui,), ("F%d" % ui,))
                ob = Fb[7][:, :].bitcast(BF16)[:, 0:TP]
                for bc in blocks:
                    pi = uid() % 2
                    tr(PS[pi][:, 0:128], Fb[fi][:, bc:bc + 128], ident[:, :], ("F%d" % fi, "ident"), ("ps%d" % pi,))
                    vi = uid() % 2
                    cp("act", vt[vi], PS[pi][:, 0:128], ("ps%d" % pi,), ("vt%d" % vi,))
                    po = 2 + uid() % 2
                    mm(PS[po][:, 0:128], vt[vi], wsT[:, g, :], True, True, ("vt%d" % vi, "wsT"), ("ps%d" % po,))
                    tt("dve", Fb[6][:, 0:128], PS[po][:, 0:128], bsb[:, g * 128:(g + 1) * 128], ALU.add, ("ps%d" % po, "bsb"), ("F6",))
                    tt("dve", ob[:, bc:bc + 128], Fb[6][:, 0:128], Fb[ui][:, bc:bc + 128], ALU.mult, ("F6", "F%d" % ui), ("F7",))
                row = (c.AQ + g) * 128
                P.dma("pool", mixT[row:row + 128, c.L0:c.L0 + TL], ob[:, c.L0:c.L0 + TL], ("F7",), ("mixT%d" % (c.AQ + g),), "stF7")
                P.dma("pool", mixT[row:row + 128, c.C0:c.C0 + TC], ob[:, c.C0:c.C0 + TC], ("F7",), ("mixT%d" % (c.AQ + g),), "stF7b")

        def conv3(dst, src, w0, w1, w2, bias, rk):
            d = Fb[dst][:, 1:TP - 1]
            if bias is None:
                ts("dve", d, Fb[src][:, 0:TP - 2], w0, None, ALU.mult, None, ("F%d" % src,) + rk, ("F%d" % dst,))
            else:
                ts("dve", d, Fb[src][:, 0:TP - 2], w0, bias, ALU.mult, ALU.add, ("F%d" % src,) + rk, ("F%d" % dst,))
            stt("dve", d, Fb[src][:, 1:TP - 1], w1, d, ALU.mult, ALU.add, ("F%d" % src, "F%d" % dst) + rk, ("F%d" % dst,))
            stt("dve", d, Fb[src][:, 2:TP], w2, d, ALU.mult, ALU.add, ("F%d" % src, "F%d" % dst) + rk, ("F%d" % dst,))

        def odd_conv(o):
            for j in range(c.CG):
                P.dma("sp", Fb[0][:, :], projT[j * 128:(j + 1) * 128, :], ("projT%d" % j,), ("F0",), "ldF0")
                P.dma("sp", Fb[1][:, :], projT[(c.CG + j) * 128:(c.CG + j + 1) * 128, :], ("projT%d" % (c.CG + j),), ("F1",), "ldF1")
                P.dma("sp", Fb[2][:, :], projT[(2 * c.CG + j) * 128:(2 * c.CG + j + 1) * 128, :], ("projT%d" % (2 * c.CG + j),), ("F2",), "ldF2")
                tt("pool", Fb[1][:, :], Fb[1][:, :], Fb[2][:, :], ALU.mult, ("F1", "F2"), ("F1",))
                wv = lambda t_: ccwT[:, (o * 3 + t_) * c.CG + j:(o * 3 + t_) * c.CG + j + 1]
                conv3(3, 1, wv(0), wv(1), wv(2), None, ("ccwT",))
                ob = Fb[7][:, :].bitcast(BF16)[:, 0:TP]
                tt("dve", ob[:, 1:TP - 1], Fb[3][:, 1:TP - 1], Fb[0][:, 1:TP - 1], ALU.mult, ("F3", "F0"), ("F7",))
                P.dma("pool", mixT[j * 128:(j + 1) * 128, 1:TP - 1], ob[:, 1:TP - 1], ("F7",), ("mixT%d" % j,), "stF7")

        def fourier():
            P.barrier(scr[:, 0:1])
            fb0 = 3 * c.CG
            fbf = Fb[6][:, :].bitcast(BF16)[:, 0:TP]
            abst = arena[:, 0:2 * c.DG * 256].rearrange("p (s w) -> p s w", s=2)
            for (ntok, col0, dstl) in ((TL, c.L0, ab_src), (TC, c.C0, None)):
                for g in range(c.DG):
                    fi = uid() % 2
                    P.dma("sp", Fb[fi][:, :], projT[(fb0 + g) * 128:(fb0 + g + 1) * 128, :], ("projT%d" % (fb0 + g),), ("F%d" % fi,), "ldF%d" % fi)
                    fbb = Fb[2 + fi][:, :].bitcast(BF16)[:, 0:TP]
                    cp("act", fbb, Fb[fi][:, :], ("F%d" % fi,), ("F%d" % (2 + fi),))
                    for tb in range(0, ntok, 128):
                        pi = uid() % 2
                        mm(PS[pi][:, 0:256], fbb[:, col0 + tb:col0 + tb + 128], cs_bf[:, :], True, True, ("F%d" % (2 + fi), "cs_bf"), ("ps%d" % pi,))
                        ai = uid() % 2
                        cp("dve", abst[:, ai, 0:256], PS[pi][:, 0:256], ("ps%d" % pi,), ("abst%d" % ai,))
                        if dstl is None:
                            P.dma("pool", ab_ctx[tb:tb + 128, g * 256:(g + 1) * 256], abst[:, ai, 0:256], ("abst%d" % ai,), ("ab_ctx",), "sta%d" % ai)
                        else:
                            P.dma("pool", dstl[tb // 128][g // GH][:, (g % GH) * 256:(g % GH + 1) * 256], abst[:, ai, 0:256],
                                  ("abst%d" % ai,), ("ab_src",), "sta%d" % ai)
            for i in range(NB):
                for gh in range(NGH):
                    P.op("pool", lambda en, a=ab_src[i][gh], b=ab_all[i][gh]: en.collective_compute(
                        "AllGather", ALU.bypass, replica_groups=[list(range(NCORES))], ins=[a[:, :]], outs=[b[:, :]]),
                        ("ab_src",), ("ab_all",), cc=True)
            P.barrier(scr[:, 0:1])
            for (nt, abd, dc, ds_, col0, nk) in ((c.SEQ // 128, None, dftc, dfts, c.L0, TL), (TC // 128, ab_ctx, dftcc, dftcs, c.C0, TC)):
                KT = 128
                off = 0
                dC = arena[:, off:off + nt * KT].rearrange("p (n k) -> p n k", k=KT); off += nt * KT
                dS = arena[:, off:off + nt * KT].rearrange("p (n k) -> p n k", k=KT); off += nt * KT
                abg = [arena[:, off:off + nt * 256].rearrange("p (n w) -> p n w", w=256)] * 2
                off += nt * 256
                assert off <= AR_N, (off, AR_N)
                for k0 in range(0, nk, KT):
                    P.dma("sp", dC, dc.ap().rearrange("(n p) k -> p n k", p=128)[:, :, k0:k0 + KT], (), ("dC",), "lddc")
                    P.dma("sp", dS, ds_.ap().rearrange("(n p) k -> p n k", p=128)[:, :, k0:k0 + KT], (), ("dS",), "ldds")
                    for g in range(c.DG):
                        gi = 0
                        if abd is None:
                            for i in range(NB):
                                P.dma("sp", abg[gi][:, i * NCORES:(i + 1) * NCORES, :],
                                      ab_all[i][g // GH].ap().rearrange("(r p) w -> p r w", p=128)[:, :, (g % GH) * 256:(g % GH + 1) * 256],
                                      ("ab_all",), ("abg%d" % gi,), "ldab%d" % gi)
                        else:
                            P.dma("sp", abg[gi], abd.ap().rearrange("(n p) w -> p n w", p=128)[:, :, g * 256:(g + 1) * 256], ("ab_ctx",), ("abg%d" % gi,), "ldab%d" % gi)
                        pi = uid() % 2
                        for n in range(nt):
                            mm(PS[pi][:, 0:KT], abg[gi][:, n, 0:128], dC[:, n, :], n == 0, False, ("abg%d" % gi, "dC"), ("ps%d" % pi,))
                            mm(PS[pi][:, 0:KT], abg[gi][:, n, 128:256], dS[:, n, :], False, n == nt - 1, ("abg%d" % gi, "dS"), ("ps%d" % pi,))
                        oi = 4 + uid() % 2
                        ob = Fb[oi][:, :].bitcast(BF16)[:, 0:KT]
                        cp("act", ob, PS[pi][:, 0:KT], ("ps%d" % pi,), ("F%d" % oi,))
                        row = (c.CG + g) * 128
                        P.dma("pool", mixT[row:row + 128, col0 + k0:col0 + k0 + KT], ob, ("F%d" % oi,), ("mixT%d" % (c.CG + g),), "stF%d" % oi)
            P.barrier(scr[:, 0:1])

        def ffn_gate(l):
            for j in range(c.FC):
                P.dma("sp", Fb[0][:, :], projT[j * 128:(j + 1) * 128, :], ("projT%d" % j,), ("F0",), "ldF0")
                P.dma("sp", Fb[1][:, :], projT[(c.FC + j) * 128:(c.FC + j + 1) * 128, :], ("projT%d" % (c.FC + j),), ("F1",), "ldF1")
                wv = lambda t_, jj: fcwT[:, (l * 3 + t_) * 2 * c.FC + jj:(l * 3 + t_) * 2 * c.FC + jj + 1]
                bv = lambda jj: fcbT[:, l * 2 * c.FC + jj:l * 2 * c.FC + jj + 1]
                conv3(2, 0, wv(0, j), wv(1, j), wv(2, j), bv(j), ("fcwT", "fcbT"))
                conv3(3, 1, wv(0, c.FC + j), wv(1, c.FC + j), wv(2, c.FC + j), bv(c.FC + j), ("fcwT", "fcbT"))
                act(Fb[2][:, 1:TP - 1], Fb[2][:, 1:TP - 1], AF.Silu, ("F2",), ("F2",))
                ob = Fb[7][:, :].bitcast(BF16)[:, 0:TP]
                tt("pool", ob[:, 1:TP - 1], Fb[2][:, 1:TP - 1], Fb[3][:, 1:TP - 1], ALU.mult, ("F2", "F3"), ("F7",))
                P.dma("pool", hidT[j * 128:(j + 1) * 128, 1:TP - 1], ob[:, 1:TP - 1], ("F7",), ("hidT%d" % j,), "stF7")

        halo_exchange()
        for l in range(L):
            lin_epoch[0] = l
            make_mod(l)
            norm_mod(1)
            if l % 2 == 0:
                e = l // 2
                linear(W_in_a[e], KC, inT, TT, ep_store(projT))
                P.barrier(scr[:, 0:1])
                even_post(e)
                P.barrier(scr[:, 0:1])
                gmlp(e)
                attention()
                wout = W_out_a[e]
            else:
                o = l // 2
                linear(W_in_c[o], KC, inT, TT, ep_store(projT))
                odd_conv(o)
                fourier()
                wout = W_out_c[o]
            P.barrier(scr[:, 0:1])
            load_inT(mixT, KC, inT, 0, TP)
            linear(wout, KC, inT, TT, ep_resid(2))
            P.barrier(scr[:, 0:1])
            halo_exchange()
            norm_mod(2)
            linear(W_up[l], KC, inT, TT, ep_store(projT))
            ffn_gate(l)
            P.barrier(scr[:, 0:1])
            half = (TP + 1) // 2
            for (lo, hi) in ((1, half), (half, TP - 1)):
                hv = arena[:, 0:c.FC * (hi - lo)].rearrange("p (k t) -> p k t", k=c.FC)
                load_inT(hidT, c.FC, hv, lo, hi)
                tl_ = [(a - lo, b - lo) for (a, b) in ctiles(lo, hi)]

                def ep(m, tiles, lo=lo):
                    ep_resid(5)(m, [(pi, a + lo, b + lo) for (pi, a, b) in tiles])
                linear(W_down[l], c.FC, hv, tl_, ep)
                P.barrier(scr[:, 0:1])
            if l < L - 1:
                halo_exchange()

        for tb in range(0, TL, 128):
            for kb in range(0, KC, 2):
                fi = uid() % 2
                for k in range(2):
                    P.dma("sp", Fb[2 + k][:, 0:128], xT[(kb + k) * 128:(kb + k + 1) * 128, c.L0 + tb:c.L0 + tb + 128],
                          ("xT%d" % (kb + k),), ("F%d" % (2 + k),), "ldF%d" % (2 + k))
                    pi = uid() % 2
                    tr(PS[pi][:, 0:128], Fb[2 + k][:, 0:128], ident[:, :], ("F%d" % (2 + k), "ident"), ("ps%d" % pi,))
                    cp("dve" if k % 2 else "act", Fb[fi][:, k * 128:(k + 1) * 128], PS[pi][:, 0:128], ("ps%d" % pi,), ("F%d" % fi,))
                P.dma("pool", y_out[tb:tb + 128, kb * 128:kb * 128 + 256], Fb[fi][:, 0:256], ("F%d" % fi,), ("y",), "stF%d" % fi)
        P.barrier(scr[:, 0:1])
        P.emit(nc)
    return nc


def host_inputs(cfg, inp):
    c = cfg
    D, KC, TL, TC, TP, L = c.D, c.KC, c.TL, c.TC, c.TP, c.L
    f32 = np.float32
    bf = ml_dtypes.bfloat16
    x = np.asarray(inp["x"], f32).reshape(c.SEQ, D)
    ctx = np.ascontiguousarray(np.asarray(inp["ctx"], f32).reshape(TC, D))
    cvec = np.concatenate([np.asarray(inp["c"], f32).reshape(KC, 128), np.asarray(inp["c_ctx"], f32).reshape(KC, 128)], 0)
    rows = c.SEQ // c.GRID_W
    row = np.repeat(np.arange(rows, dtype=f32), c.GRID_W)
    col = np.tile(np.arange(c.GRID_W, dtype=f32), rows)
    half = 64
    inv_freq = (ROPE_THETA ** (-np.arange(0, half, 2, dtype=f32) / half)).astype(f32)
    ang = np.concatenate([row[:, None] * inv_freq, col[:, None] * inv_freq], -1).astype(f32)
    cos = np.repeat(np.cos(ang).astype(f32), 2, axis=1)
    sin = np.repeat(np.sin(ang).astype(f32), 2, axis=1)
    pm = np.zeros((128, 128), f32)
    for i in range(64):
        pm[2 * i, 2 * i + 1] = -1.0
        pm[2 * i + 1, 2 * i] = 1.0
    pmT = np.ascontiguousarray(pm.T)
    cc = np.arange(128)
    angc = 2 * np.pi * np.outer(cc, cc) / 128.0
    csmat = np.concatenate([np.cos(angc), -np.sin(angc)], 1).astype(f32)
    t = np.arange(c.SEQ, dtype=np.int64)
    scl = 1.0 / np.sqrt(c.SEQ * 128.0)
    tcx = np.arange(TC, dtype=np.int64)
    angx = 2 * np.pi * ((np.outer(tcx, tcx)) % TC) / TC
    sclx = 1.0 / np.sqrt(TC * 128.0)
    dft_cc = (np.cos(angx) * sclx).astype(bf); dft_cs = (np.sin(angx) * sclx).astype(bf)
    ident = np.eye(128, dtype=f32)

    def rs(a, n):
        a = np.asarray(a, f32)
        k = a.shape[0] // NCORES
        return [np.ascontiguousarray(a[r * k:(r + 1) * k]) for r in range(NCORES)]
    maps = [dict() for _ in range(NCORES)]
    wada = np.asarray(inp["w_ada"], f32); bada = np.asarray(inp["b_ada"], f32)
    aw = c.ACH * 128
    for r in range(NCORES):
        m = maps[r]
        m["x_loc"] = np.ascontiguousarray(x[r * TL:(r + 1) * TL]); m["ctx"] = ctx; m["cvec"] = cvec
        m["w_ada"] = np.ascontiguousarray(wada[:, :, r * aw:(r + 1) * aw]).reshape(L * D, aw)
        m["b_ada"] = np.ascontiguousarray(bada[:, r * aw:(r + 1) * aw]).reshape(1, L * aw)
        m["norm1_g"] = np.asarray(inp["norm1_g"], f32).reshape(L * KC, 128)
        m["norm2_g"] = np.asarray(inp["norm2_g"], f32).reshape(L * KC, 128)
        m["ffn_conv_w"] = np.asarray(inp["ffn_conv_w"], f32).reshape(-1, 128)
        m["ffn_conv_b"] = np.asarray(inp["ffn_conv_b"], f32).reshape(-1, 128)
        m["conv_w_c"] = np.asarray(inp["conv_w_c"], f32).reshape(-1, 128)
        m["gmlp_norm_g"] = np.asarray(inp["gmlp_norm_g"], f32).reshape(-1, 128)
        m["qk_norm_g"] = np.concatenate([np.asarray(inp["q_norm_g"], f32), np.asarray(inp["k_norm_g"], f32)], 0)
        m["gmlp_w_s"] = np.asarray(inp["gmlp_w_s"], f32).reshape(-1, 128)
        m["gmlp_b_s"] = np.asarray(inp["gmlp_b_s"], f32).reshape(c.NE, -1)
        rc = np.ones((128, TP), f32); rsn = np.zeros((128, TP), f32)
        rc[:, c.L0:c.L0 + TL] = cos[r * TL:(r + 1) * TL].T; rsn[:, c.L0:c.L0 + TL] = sin[r * TL:(r + 1) * TL].T
        m["rope_cos"] = rc; m["rope_sin"] = rsn
        m["csmat"] = csmat
        kk = np.arange(r * TL, (r + 1) * TL, dtype=np.int64)
        tperm = (np.arange(NCORES)[None, :, None] * TL + np.arange(TL // 128)[:, None, None] * 128 + np.arange(128)[None, None, :]).reshape(-1)
        a_ = 2 * np.pi * ((np.outer(t[tperm], kk)) % c.SEQ) / c.SEQ
        m["dft_c"] = (np.cos(a_) * scl).astype(bf); m["dft_s"] = (np.sin(a_) * scl).astype(bf)
        m["dft_cc"] = dft_cc; m["dft_cs"] = dft_cs
        mk = np.zeros((128, 2 * NCORES + 2), f32)
        if r > 0:
            mk[:, r - 1] = 1.0; mk[:, 2 * NCORES] = 1.0
        if r < NCORES - 1:
            mk[:, NCORES + r + 1] = 1.0; mk[:, 2 * NCORES + 1] = 1.0
        m["masks"] = mk; m["ident"] = ident; m["pmat"] = pmT
    for name in ("w_in_a", "w_out_a", "w_in_c", "w_out_c", "w_ffn_up", "w_ffn_down"):
        a = np.asarray(inp[name], f32)
        for i in range(a.shape[0]):
            sh = rs(a[i], NCORES)
            for r in range(NCORES):
                maps[r]["%s%d" % (name, i)] = sh[r]
    return maps


_NC_CACHE = {}


def run(cfg, inp):
    key = (cfg.D, cfg.SEQ, cfg.CTX, cfg.GRID_W)
    if key not in _NC_CACHE:
        _NC_CACHE[key] = build(cfg)
    nc = _NC_CACHE[key]
    maps = host_inputs(cfg, inp)
    res = run_bass_kernel_spmd(nc, maps, core_ids=list(range(NCORES)))
    y = np.concatenate([np.asarray(res.results[r]["y"]) for r in range(NCORES)], 0)
    return y.reshape(1, cfg.SEQ, cfg.D).astype(np.float32)


def kernel(**inputs):
    return run(Cfg(), inputs)
```
